# Optimizing a Trainium2 kernel written in Bass

```python
import math
import jax
import jax.numpy as jnp
from jax import lax
import numpy as np

D_MODEL = 1024
BATCH = 8
SEQ = 8192
DEPTH = 2

GRID_W = 64
CTX_LEN = 256
EPS = 1e-6
DN_HEADS = 4
DN_DK = 128
DN_DV = 128
DN_KEY = DN_HEADS * DN_DK
DN_VAL = DN_HEADS * DN_DV
DN_CONV = 3
CHUNK = 64
SC_WIDTH = 512
SC_CONV = 3
FN_GROUPS = 4
FN_GROUP_DIM = 128
FN_WIDTH = FN_GROUPS * FN_GROUP_DIM
N_BRANCH = 3
D_FF = ((8 * D_MODEL // 3 + 255) // 256) * 256
IN_SIZES = (2 * DN_KEY + DN_VAL, DN_VAL, 4 * DN_HEADS, 3 * SC_WIDTH, FN_WIDTH)
N_IN = 2 * DN_KEY + DN_VAL + DN_VAL + 4 * DN_HEADS + 3 * SC_WIDTH + FN_WIDTH

kernel_name = "hybrid_gdn_shortconv_fnet_dit"


def split_cols(p, sizes):
    out = []
    start = 0
    for s in sizes:
        out.append(p[..., start:start + s])
        start += s
    return out


def rmsnorm(t, gain):
    tf = t.astype(jnp.float32)
    y = tf * lax.rsqrt(jnp.mean(tf * tf, axis=-1, keepdims=True) + EPS) * gain.astype(jnp.float32)
    return y.astype(t.dtype)


def modulate(t, gain, shift, scale):
    return rmsnorm(t, gain) * (1 + scale) + shift


def l2norm(t):
    tf = t.astype(jnp.float32)
    return tf * lax.rsqrt(jnp.sum(tf * tf, axis=-1, keepdims=True) + EPS)


def dwconv(u, w):
    k = w.shape[0]
    return lax.conv_general_dilated(
        u, w[:, None, :].astype(u.dtype), window_strides=(1,), padding=[(k // 2, k // 2)],
        dimension_numbers=("NWC", "WIO", "NWC"), feature_group_count=u.shape[-1])


def conv_seq(u, w):
    return dwconv(u, w)


def conv_grid_rows(u, w):
    bsz, length, ch = u.shape
    rows = length // GRID_W
    return dwconv(u.reshape(bsz * rows, GRID_W, ch), w).reshape(bsz, length, ch)


def gated_delta_chunked(q, k, v, beta, g, s0):
    bsz, length, heads, _ = q.shape
    dv = v.shape[-1]
    n = length // CHUNK

    def blocks(t):
        t = t.astype(jnp.float32).reshape((bsz, n, CHUNK, heads) + t.shape[3:])
        return jnp.moveaxis(t, 3, 1)

    q, k, v, beta, g = (blocks(t) for t in (q, k, v, beta, g))
    g = jnp.cumsum(g, axis=-1)
    idx = jnp.arange(CHUNK)
    lower = idx[:, None] >= idx[None, :]
    strict = idx[:, None] > idx[None, :]
    decay = jnp.exp(jnp.where(lower, g[..., :, None] - g[..., None, :], -jnp.inf))
    kb = k * beta[..., None]
    a = jnp.where(strict, jnp.einsum("bhnid,bhnjd->bhnij", kb, k) * decay, 0.0)
    eye = jnp.eye(CHUNK, dtype=jnp.float32)
    t_inv = lax.linalg.triangular_solve(a + eye, jnp.broadcast_to(eye, a.shape), left_side=True,
                                        lower=True, unit_diagonal=True)
    u = t_inv @ (v * beta[..., None])
    w = t_inv @ (kb * jnp.exp(g)[..., None])
    attn = jnp.einsum("bhnid,bhnjd->bhnij", q, k) * decay
    qg = q * jnp.exp(g)[..., None]
    g_last = g[..., -1]
    kd = k * jnp.exp(g_last[..., None] - g)[..., None]

    def step(s, inp):
        u_i, w_i, qg_i, kd_i, attn_i, gl_i = inp
        v_new = u_i - jnp.einsum("bhcd,bhde->bhce", w_i, s)
        o_i = jnp.einsum("bhcd,bhde->bhce", qg_i, s) + jnp.einsum("bhij,bhje->bhie", attn_i, v_new)
        s = s * jnp.exp(gl_i)[..., None, None] + jnp.einsum("bhcd,bhce->bhde", kd_i, v_new)
        return s, o_i

    xs = tuple(jnp.moveaxis(t, 2, 0) for t in (u, w, qg, kd, attn, g_last))
    s_final, o = lax.scan(step, s0, xs)
    o = jnp.transpose(o, (1, 0, 3, 2, 4)).reshape(bsz, length, heads, dv)
    return o, s_final


def deltanet_prep(p_qkv, p_ba, conv_fn, conv_w, a_log, dt_bias):
    qkv = jax.nn.silu(conv_fn(p_qkv, conv_w))
    bsz, length, _ = qkv.shape
    q, k, v = split_cols(qkv, (DN_KEY, DN_KEY, DN_VAL))
    q = l2norm(q.reshape(bsz, length, DN_HEADS, DN_DK)) * (DN_DK ** -0.5)
    k = l2norm(k.reshape(bsz, length, DN_HEADS, DN_DK))
    v = v.reshape(bsz, length, DN_HEADS, DN_DV)
    ba = p_ba.astype(jnp.float32)
    beta = jax.nn.sigmoid(ba[..., :2 * DN_HEADS]).reshape(bsz, length, 2, DN_HEADS)
    alpha = ba[..., 2 * DN_HEADS:].reshape(bsz, length, 2, DN_HEADS)
    g = -jnp.exp(a_log.astype(jnp.float32)) * jax.nn.softplus(alpha + dt_bias.astype(jnp.float32))
    return q, k, v, beta, g


def bidir_gated_delta(dn_ctx, dn_lat):
    qc, kc, vc, bc, gc = dn_ctx
    ql, kl, vl, bl, gl = dn_lat
    s0 = jnp.zeros((qc.shape[0], DN_HEADS, DN_DK, DN_DV), jnp.float32)
    outs_c = []
    outs_l = []
    for d in range(2):
        rev = (lambda t: jnp.flip(t, axis=1)) if d == 1 else (lambda t: t)
        oc, s_ctx = gated_delta_chunked(rev(qc), rev(kc), rev(vc), rev(bc[:, :, d]), rev(gc[:, :, d]), s0)
        ol, _ = gated_delta_chunked(rev(ql), rev(kl), rev(vl), rev(bl[:, :, d]), rev(gl[:, :, d]), s_ctx)
        outs_c.append(rev(oc))
        outs_l.append(rev(ol))
    return (outs_c[0] + outs_c[1]).astype(vc.dtype), (outs_l[0] + outs_l[1]).astype(vl.dtype)


def fourier_mix(u):
    bsz, length, _ = u.shape
    uf = u.astype(jnp.float32).reshape(bsz, length, FN_GROUPS, FN_GROUP_DIM)
    y = jnp.fft.fftn(uf, axes=(1, 3), norm="ortho").real
    return y.reshape(bsz, length, FN_WIDTH).astype(u.dtype)


def branch_merge(h, o_dn, z, sc_p, fn_u, conv_fn, onorm_g, sc_conv_w, w_dn_out, w_sc_out, w_fn_out,
                 w_gate, b_gate, w_o):
    bsz, length, _ = h.shape
    zf = jax.nn.silu(z.reshape(bsz, length, DN_HEADS, DN_DV))
    y_dn = (rmsnorm(o_dn, onorm_g) * zf).reshape(bsz, length, DN_VAL) @ w_dn_out
    sb, scc, sx = split_cols(sc_p, (SC_WIDTH, SC_WIDTH, SC_WIDTH))
    y_sc = (sb * conv_fn(scc * sx, sc_conv_w)) @ w_sc_out
    y_fn = fourier_mix(fn_u) @ w_fn_out
    gates = jax.nn.sigmoid((h @ w_gate + b_gate).astype(jnp.float32)).astype(h.dtype)
    gates = gates.reshape(bsz, length, N_BRANCH, D_MODEL)
    merged = gates[:, :, 0] * y_dn + gates[:, :, 1] * y_sc + gates[:, :, 2] * y_fn
    return merged @ w_o


def swiglu(h, w_ffn_in, w_ffn_out):
    a, b = split_cols(h @ w_ffn_in, (D_FF, D_FF))
    return (jax.nn.silu(a) * b) @ w_ffn_out


def setup_inputs(seed: int = 0) -> dict:
    key = jax.random.key(seed)
    ks = jax.random.split(key, 24)
    f32 = jnp.float32

    def nrm(k, shape, fan_in, s=1.0):
        return jax.random.normal(k, shape, f32) * (s * fan_in ** -0.5)

    def gain(k, shape):
        return 1.0 + 0.02 * jax.random.normal(k, shape, f32)

    dt = jnp.exp(jax.random.uniform(ks[10], (DEPTH, 2, DN_HEADS), f32, math.log(1e-3), math.log(1e-1)))
    return {
        "x": jax.random.normal(ks[0], (BATCH, SEQ, D_MODEL), f32),
        "c": jax.random.normal(ks[1], (BATCH, D_MODEL), f32),
        "ctx": jax.random.normal(ks[2], (BATCH, CTX_LEN, D_MODEL), f32),
        "c_ctx": jax.random.normal(ks[3], (D_MODEL,), f32),
        "w_mod": nrm(ks[4], (DEPTH, D_MODEL, 6 * D_MODEL), D_MODEL, 0.5),
        "b_mod": 0.02 * jax.random.normal(ks[5], (DEPTH, 6 * D_MODEL), f32),
        "norm1_g": gain(ks[6], (DEPTH, D_MODEL)),
        "w_in": nrm(ks[7], (DEPTH, D_MODEL, N_IN), D_MODEL),
        "dn_conv_w": nrm(ks[8], (DEPTH, DN_CONV, 2 * DN_KEY + DN_VAL), DN_CONV),
        "dn_a_log": jnp.log(jax.random.uniform(ks[9], (DEPTH, 2, DN_HEADS), f32, 1.0, 16.0)),
        "dn_dt_bias": jnp.log(jnp.expm1(dt)),
        "dn_onorm_g": gain(ks[11], (DEPTH, DN_DV)),
        "w_dn_out": nrm(ks[12], (DEPTH, DN_VAL, D_MODEL), DN_VAL),
        "sc_conv_w": nrm(ks[13], (DEPTH, SC_CONV, SC_WIDTH), SC_CONV),
        "w_sc_out": nrm(ks[14], (DEPTH, SC_WIDTH, D_MODEL), SC_WIDTH),
        "w_fn_out": nrm(ks[15], (DEPTH, FN_WIDTH, D_MODEL), FN_WIDTH),
        "w_gate": nrm(ks[16], (DEPTH, D_MODEL, N_BRANCH * D_MODEL), D_MODEL),
        "b_gate": 0.02 * jax.random.normal(ks[17], (DEPTH, N_BRANCH * D_MODEL), f32),
        "w_o": nrm(ks[18], (DEPTH, D_MODEL, D_MODEL), D_MODEL),
        "norm2_g": gain(ks[19], (DEPTH, D_MODEL)),
        "w_ffn_in": nrm(ks[20], (DEPTH, D_MODEL, 2 * D_FF), D_MODEL),
        "w_ffn_out": nrm(ks[21], (DEPTH, D_FF, D_MODEL), D_FF),
        "final_g": gain(ks[22], (D_MODEL,)),
    }


def reference(x, c, ctx, c_ctx, w_mod, b_mod, norm1_g, w_in, dn_conv_w, dn_a_log, dn_dt_bias,
              dn_onorm_g, w_dn_out, sc_conv_w, w_sc_out, w_fn_out, w_gate, b_gate, w_o, norm2_g,
              w_ffn_in, w_ffn_out, final_g):
    xc = ctx
    for layer in range(DEPTH):
        last = layer == DEPTH - 1
        mod = jax.nn.silu(c) @ w_mod[layer] + b_mod[layer]
        sh1, sc1, g1, sh2, sc2, g2 = jnp.split(mod[:, None, :], 6, axis=-1)
        mod_c = jax.nn.silu(c_ctx) @ w_mod[layer] + b_mod[layer]
        csh1, csc1, cg1, csh2, csc2, cg2 = jnp.split(mod_c, 6)

        h = modulate(x, norm1_g[layer], sh1, sc1)
        hc = modulate(xc, norm1_g[layer], csh1, csc1)
        qkv, z, ba, sc_p, fn_u = split_cols(h @ w_in[layer], IN_SIZES)
        qkv_c, z_c, ba_c, sc_pc, fn_uc = split_cols(hc @ w_in[layer], IN_SIZES)
        dn_lat = deltanet_prep(qkv, ba, conv_grid_rows, dn_conv_w[layer], dn_a_log[layer], dn_dt_bias[layer])
        dn_ctx = deltanet_prep(qkv_c, ba_c, conv_seq, dn_conv_w[layer], dn_a_log[layer], dn_dt_bias[layer])
        o_ctx, o_lat = bidir_gated_delta(dn_ctx, dn_lat)
        mix = branch_merge(h, o_lat, z, sc_p, fn_u, conv_grid_rows, dn_onorm_g[layer], sc_conv_w[layer],
                           w_dn_out[layer], w_sc_out[layer], w_fn_out[layer], w_gate[layer],
                           b_gate[layer], w_o[layer])
        x = x + g1 * mix
        if not last:
            mix_c = branch_merge(hc, o_ctx, z_c, sc_pc, fn_uc, conv_seq, dn_onorm_g[layer],
                                 sc_conv_w[layer], w_dn_out[layer], w_sc_out[layer], w_fn_out[layer],
                                 w_gate[layer], b_gate[layer], w_o[layer])
            xc = xc + cg1 * mix_c
            xc = xc + cg2 * swiglu(modulate(xc, norm2_g[layer], csh2, csc2), w_ffn_in[layer], w_ffn_out[layer])

        x = x + g2 * swiglu(modulate(x, norm2_g[layer], sh2, sc2), w_ffn_in[layer], w_ffn_out[layer])
    return rmsnorm(x, final_g)
```

```python
import math
from contextlib import ExitStack
import numpy as np
import concourse.bass as bass
import concourse.mybir as mybir
from concourse.bass_utils import run_bass_kernel_spmd

F32 = mybir.dt.float32
F32R = mybir.dt.float32r
BF16 = mybir.dt.bfloat16
FAST_MM = True
MM_BF16 = True
MMT = BF16 if (FAST_MM and MM_BF16) else F32
ALU = mybir.AluOpType
AF = mybir.ActivationFunctionType

D = 1024
KC = 8
SEQ = 8192
CTX = 256
T = SEQ + CTX
DEPTH = 2
NB = 512
NIN = 4112
DFF = 2816
FC = 22
EPS = 1e-6
BIG = 30000.0
NCORES = 8


class Tile:
    def __init__(self, ctx, handle, name):
        self.ctx = ctx
        self.h = handle
        self.name = name
        self.w = None
        self.r = {}

    def __getitem__(self, idx):
        return TV(self, self.h[idx])

    def full(self):
        return TV(self, self.h[:])


class TV:
    def __init__(self, tile, ap):
        self.t = tile
        self.ap = ap

    def re(self, pat, **kw):
        return TV(self.t, self.ap.rearrange(pat, **kw))

    def r(self):
        if not FAST_MM or self.ap.dtype != F32:
            return self
        return TV(self.t, self.ap.bitcast(F32R))

    def __getitem__(self, idx):
        return TV(self.t, self.ap[idx])


def _ap(x):
    return x.ap if isinstance(x, TV) else x


class Engine:
    def __init__(self, ctx, name, eng):
        self.name = name
        self.eng = eng
        self.sem = ctx.new_sem("s_" + name)
        self.n = 0
        self.inc = 0
        self.val = {}
        self.seen = {}


class Ctx:
    def __init__(self, nc, stack, needed=None):
        self.nc = nc
        self.stack = stack
        self.needed_in = needed
        self.needed = set()
        self.sems = {}
        self.pe = Engine(self, "pe", nc.tensor)
        self.act = Engine(self, "act", nc.scalar)
        self.dve = Engine(self, "dve", nc.vector)
        self.pool = Engine(self, "pool", nc.gpsimd)
        self.sp = Engine(self, "sp", nc.sync)
        self.engs = {e.sem: e for e in (self.pe, self.act, self.dve, self.pool, self.sp)}
        self.dma_rings = {}
        for e in (self.sp, self.pool):
            ring = [self.new_sem(f"d_{e.name}{i}") for i in range(16)]
            self.dma_rings[e.name] = dict(sems=ring, vals=[0] * len(ring), i=0)
        self.dram_w = {}
        self.ntile = 0
        self.ninst = 0

    def new_sem(self, name):
        s = self.stack.enter_context(self.nc.semaphore(name))
        self.sems[name] = s
        return name

    def sbuf(self, shape, dtype=F32, name=None, stack=None):
        self.ntile += 1
        name = f"{name or 't'}_{self.ntile}"
        h = (stack or self.stack).enter_context(self.nc.sbuf_tensor(name, list(shape), dtype))
        return Tile(self, h, name)

    def psum(self, shape, dtype=F32, name=None, stack=None):
        self.ntile += 1
        name = name or f"p{self.ntile}"
        h = (stack or self.stack).enter_context(self.nc.psum_tensor(name, list(shape), dtype))
        return Tile(self, h, name)

    def _wait(self, E, ticket):
        if ticket is None:
            return
        key, val = ticket
        if E.seen.get(key, 0) >= val:
            return
        self.needed.add(ticket)
        real = self.engs[key].val[val] if key in self.engs else val
        E.eng.wait_ge(self.sems[key], real)
        E.seen[key] = val

    def _deps(self, E, reads, writes, same_ok=False):
        for tv in reads:
            if isinstance(tv, TV):
                t = tv.t
                if t.w is not None and not (same_ok and t.w[0] == E.sem):
                    self._wait(E, t.w)
        for tv in writes:
            if isinstance(tv, TV):
                t = tv.t
                if t.w is not None and not (same_ok and t.w[0] == E.sem):
                    self._wait(E, t.w)
                for key, val in t.r.items():
                    if same_ok and key == E.sem:
                        continue
                    self._wait(E, (key, val))

    def _commit(self, ticket, reads, writes):
        for tv in writes:
            if isinstance(tv, TV):
                tv.t.w = ticket
                tv.t.r = {}
        for tv in reads:
            if isinstance(tv, TV):
                t = tv.t
                if t.r.get(ticket[0], 0) < ticket[1]:
                    t.r[ticket[0]] = ticket[1]

    def op(self, E, fn, reads, writes, same_ok=False):
        self._deps(E, reads, writes, same_ok)
        inst = fn()
        E.n += 1
        ticket = (E.sem, E.n)
        if self.needed_in is None or ticket in self.needed_in:
            inst.then_inc(self.sems[E.sem], 1)
            E.inc += 1
            E.val[E.n] = E.inc
        self.ninst += 1
        self._commit(ticket, reads, writes)
        return ticket

    def dma(self, E, out, in_, dram_r=None, dram_w=None, nocast=False, **kw):
        ring = self.dma_rings[E.name]
        i = ring["i"]
        ring["i"] = (i + 1) % len(ring["sems"])
        key = ring["sems"][i]
        if ring["vals"][i] > 0:
            self._wait(E, (key, ring["vals"][i]))
        reads = [in_] if isinstance(in_, TV) else []
        writes = [out] if isinstance(out, TV) else []
        self._deps(E, reads, writes)
        if dram_r is not None:
            for nm in (dram_r if isinstance(dram_r, (list, tuple)) else [dram_r]):
                for tk in self.dram_w.get(nm, []):
                    self._wait(E, tk)
        oa, ia = _ap(out), _ap(in_)
        if oa.dtype == F32R and ia.dtype != F32R and not nocast:
            ia = ia.bitcast(F32R)
        inst = E.eng.dma_start(out=oa, in_=ia, **kw)
        ring["vals"][i] += 16
        inst.then_inc(self.sems[key], 16)
        ticket = (key, ring["vals"][i])
        self.ninst += 1
        self._commit(ticket, reads, writes)
        if dram_w is not None:
            lst = self.dram_w.setdefault(dram_w, [])
            lst[:] = [tk for tk in lst if tk[0] != key] + [ticket]
        return ticket

    def load(self, out, in_, dram_r=None, **kw):
        return self.dma(self.sp, out, in_, dram_r=dram_r, **kw)

    def store(self, out, in_, dram_w=None, **kw):
        return self.dma(self.sp, out, in_, dram_w=dram_w, **kw)

    def load_r(self, out, in_, dram_r=None, **kw):
        if not FAST_MM:
            return self.dma(self.pool, out, in_, dram_r=dram_r, **kw)
        return self.dma(self.pool, out.r(), in_, dram_r=dram_r, nocast=True, **kw)

    def mm(self, out, lhsT, rhs, start=True, stop=True, r=False):
        if r and FAST_MM and _ap(lhsT).dtype == F32:
            la, ra = _ap(lhsT).bitcast(F32R), _ap(rhs).bitcast(F32R)
        else:
            la, ra = _ap(lhsT), _ap(rhs)
        return self.op(self.pe, lambda: self.nc.tensor.matmul(_ap(out), la, ra, start=start, stop=stop),
                       [lhsT, rhs], [out], same_ok=True)

    def transpose(self, out, in_, ident):
        return self.op(self.pe, lambda: self.nc.tensor.transpose(_ap(out), _ap(in_), _ap(ident)),
                       [in_, ident], [out], same_ok=True)

    def act_fn(self, out, in_, func, bias=None, scale=None):
        kw = {}
        reads = [in_]
        if bias is not None:
            kw["bias"] = _ap(bias)
            reads.append(bias)
        if scale is not None:
            kw["scale"] = _ap(scale)
            reads.append(scale)
        return self.op(self.act, lambda: self.nc.scalar.activation(_ap(out), _ap(in_), func, **kw), reads, [out])

    def _ve(self, E):
        return self.nc.vector if E is self.dve else self.nc.gpsimd

    def tt(self, out, in0, in1, op, E=None):
        E = E or self.dve
        return self.op(E, lambda: self._ve(E).tensor_tensor(_ap(out), _ap(in0), _ap(in1), op), [in0, in1], [out])

    def ts(self, out, in0, s1, op0, s2=None, op1=None, E=None):
        E = E or self.dve
        reads = [in0] + [s for s in (s1, s2) if isinstance(s, TV)]
        if op1 is None:
            return self.op(E, lambda: self._ve(E).tensor_scalar(_ap(out), _ap(in0), _ap(s1), None, op0), reads, [out])
        return self.op(E, lambda: self._ve(E).tensor_scalar(_ap(out), _ap(in0), _ap(s1), _ap(s2), op0, op1), reads, [out])

    def stt(self, out, in0, scalar, in1, op0, op1, E=None):
        E = E or self.dve
        reads = [in0, in1] + ([scalar] if isinstance(scalar, TV) else [])
        return self.op(E, lambda: self._ve(E).scalar_tensor_tensor(_ap(out), _ap(in0), _ap(scalar), _ap(in1), op0, op1),
                       reads, [out])

    def copy(self, out, in_, E=None):
        E = E or self.dve
        if E is self.act:
            return self.op(E, lambda: self.nc.scalar.copy(_ap(out), _ap(in_)), [in_], [out])
        return self.op(E, lambda: self._ve(E).tensor_copy(_ap(out), _ap(in_)), [in_], [out])

    def recip(self, out, in_):
        return self.op(self.dve, lambda: self.nc.vector.reciprocal(_ap(out), _ap(in_)), [in_], [out])

    def memset(self, out, val, E=None):
        E = E or self.dve
        return self.op(E, lambda: self._ve(E).memset(_ap(out), val), [], [out])

    def barrier(self):
        tickets = []
        for name, ring in self.dma_rings.items():
            for key, val in zip(ring["sems"], ring["vals"]):
                if val > 0:
                    tickets.append((key, val))
        for e in (self.pe, self.act, self.dve, self.pool):
            if e.n > 0:
                tickets.append((e.sem, e.n))
        for E in (self.pe, self.act, self.dve, self.pool, self.sp):
            for tk in tickets:
                if tk[0] != E.sem:
                    self._wait(E, tk)

    def finish(self):
        E = self.sp
        for name, ring in self.dma_rings.items():
            for key, val in zip(ring["sems"], ring["vals"]):
                if val > 0:
                    self._wait(E, (key, val))
        for e in (self.pe, self.act, self.dve, self.pool):
            if e.n > 0:
                self._wait(E, (e.sem, e.n))


def make_consts():
    c = {}
    c["ident"] = np.eye(128, dtype=np.float32)
    c["ones"] = np.ones((128, 128), np.float32)
    idx = np.arange(128)
    same = (idx[:, None] // 64) == (idx[None, :] // 64)
    c["Uf"] = (same & (idx[:, None] <= idx[None, :])).astype(np.float32)
    c["Ub"] = (same & (idx[:, None] >= idx[None, :])).astype(np.float32)
    c["NSf"] = np.where(same & (idx[:, None] > idx[None, :]), 0.0, -BIG).astype(np.float32)
    c["NSb"] = np.where(same & (idx[:, None] < idx[None, :]), 0.0, -BIG).astype(np.float32)
    i64 = np.arange(64)
    c["NIf"] = np.where(i64[None, :] >= i64[:, None], 0.0, -BIG).astype(np.float32)
    c["NIb"] = np.where(i64[None, :] <= i64[:, None], 0.0, -BIG).astype(np.float32)
    sel = np.zeros((128, 2, 128), np.float32)
    sel[:64, 0, :] = 1.0
    sel[64:, 1, :] = 1.0
    c["SelC"] = sel
    n = np.arange(128)
    ang = 2 * np.pi * np.outer(n, n) / 128.0
    Cc, Sc = np.cos(ang), np.sin(ang)
    sl = 1.0 / math.sqrt(SEQ * 128.0)
    c["R0"] = (np.concatenate([Cc, -Sc, -Sc, -Cc], 1) * sl).astype(np.float32)
    sc_ = 1.0 / math.sqrt(CTX * 128.0)
    c["R0c"] = (np.concatenate([Cc, -Sc], 1) * sc_).astype(np.float32)
    c["C128"] = Cc.astype(np.float32)
    c["S128"] = Sc.astype(np.float32)
    N2 = SEQ // 128
    k1 = np.arange(128)[:, None]
    n2 = np.arange(N2)[None, :]
    th = 2 * np.pi * k1 * n2 / float(SEQ)
    c["twr"] = np.cos(th).astype(np.float32)
    c["twi"] = (-np.sin(th)).astype(np.float32)
    c["ntwi"] = np.sin(th).astype(np.float32)
    k2 = np.arange(N2)[None, :]
    n2c = np.arange(N2)[:, None]
    a2 = 2 * np.pi * n2c * k2 / float(N2)
    c["G64"] = np.concatenate([np.cos(a2), np.sin(a2)], 0).astype(np.float32)
    m = np.arange(256)
    a3 = 2 * np.pi * np.outer(m, m) / 256.0
    c["C256"] = np.cos(a3).astype(np.float32)
    c["S256"] = np.sin(a3).astype(np.float32)
    return c


CONST_SHAPES = {k: v.shape for k, v in make_consts().items()}


def configure(seq):
    global SEQ, T, CONST_SHAPES
    SEQ = seq
    T = SEQ + CTX
    CONST_SHAPES = {k: v.shape for k, v in make_consts().items()}

WIN_OFFS = (0, 512, 1024, 1536, 2064, 2576, 3088, 3600)

WEIGHT_SHAPES = {
    "w_mod": (DEPTH, D, 6 * D), "bmodT": (DEPTH, 128, 48), "n1gT": (DEPTH, 128, 8), "n2gT": (DEPTH, 128, 8),
    "w_in_t": (DEPTH, 8, 128, 8, 512), "w_ba": (DEPTH, 128, 8, 16),
    "dncwT": (DEPTH, 128, 12, 3), "dn_neg_alog": (DEPTH, 128, 8), "dn_dtb": (DEPTH, 128, 8),
    "onormT": (DEPTH, 128, 1), "w_dn_t": (DEPTH, 128, 8, 512), "sccwT": (DEPTH, 128, 4, 3), "w_sc_t": (DEPTH, 128, 4, D),
    "w_fn_t": (DEPTH, 128, 8, 512), "w_gate_t": (DEPTH, 6, 128, 8, 512), "bgateT": (DEPTH, 128, 24), "w_o_t": (DEPTH, 2, 128, 8, 512),
    "w_ffn_in_t": (DEPTH, FC // 2, 128, 8, 512), "w_ffn_out_t": (DEPTH, 8, 128, FC, 128), "fingT": (128, 8),
}
R_WEIGHTS = ("w_in_t", "w_dn_t", "w_sc_t", "w_fn_t", "w_gate_t", "w_o_t", "w_ffn_in_t", "w_ffn_out_t")


def host_layout(inp):
    f = lambda a: np.ascontiguousarray(a, dtype=np.float32)
    fm = lambda v, nch: f(np.asarray(v).reshape(nch, 128).T)

    def ktile(wm):
        wm = np.asarray(wm)
        return wm.reshape(wm.shape[0] // 128, 128, wm.shape[1]).transpose(1, 0, 2)

    def halves(wm):
        t = ktile(wm)
        return np.concatenate([t[:, :, 0:512], t[:, :, 512:1024]], 1)

    w = {}
    L = range(DEPTH)
    w["w_mod"] = f(inp["w_mod"])
    w["bmodT"] = f(np.stack([fm(inp["b_mod"][l], 48) for l in L]))
    w["n1gT"] = f(np.stack([fm(inp["norm1_g"][l], 8) for l in L]))
    w["n2gT"] = f(np.stack([fm(inp["norm2_g"][l], 8) for l in L]))
    w["w_in_t"] = f(np.stack([np.stack([ktile(inp["w_in"][l][:, o:o + 512]) for o in WIN_OFFS]) for l in L]))
    w["w_ba"] = f(np.stack([ktile(inp["w_in"][l][:, 2048:2064]) for l in L]))
    w["dncwT"] = f(np.stack([np.asarray(inp["dn_conv_w"][l]).T.reshape(12, 128, 3).transpose(1, 0, 2) for l in L]))
    w["dn_neg_alog"] = f(np.stack([np.broadcast_to(np.asarray(inp["dn_a_log"][l]).reshape(1, 8), (128, 8)) for l in L]))
    w["dn_dtb"] = f(np.stack([np.broadcast_to(np.asarray(inp["dn_dt_bias"][l]).reshape(1, 8), (128, 8)) for l in L]))
    w["onormT"] = f(np.stack([np.asarray(inp["dn_onorm_g"][l]).reshape(128, 1) for l in L]))
    w["w_dn_t"] = f(np.stack([halves(inp["w_dn_out"][l]) for l in L]))
    w["sccwT"] = f(np.stack([np.asarray(inp["sc_conv_w"][l]).T.reshape(4, 128, 3).transpose(1, 0, 2) for l in L]))
    w["w_sc_t"] = f(np.stack([ktile(inp["w_sc_out"][l]) for l in L]))
    w["w_fn_t"] = f(np.stack([halves(inp["w_fn_out"][l]) for l in L]))
    w["w_gate_t"] = f(np.stack([np.stack([ktile(inp["w_gate"][l][:, g * 512:(g + 1) * 512]) for g in range(6)]) for l in L]))
    w["bgateT"] = f(np.stack([fm(inp["b_gate"][l], 24) for l in L]))
    w["w_o_t"] = f(np.stack([np.stack([ktile(inp["w_o"][l][:, g * 512:(g + 1) * 512]) for g in range(2)]) for l in L]))
    w["w_ffn_in_t"] = f(np.stack([np.stack([np.concatenate([ktile(inp["w_ffn_in"][l][:, g * 256:(g + 1) * 256]),
                                                            ktile(inp["w_ffn_in"][l][:, DFF + g * 256:DFF + (g + 1) * 256])], 2)
                                            for g in range(FC // 2)]) for l in L]))
    w["w_ffn_out_t"] = f(np.stack([np.stack([ktile(inp["w_ffn_out"][l][:, m * 128:(m + 1) * 128]) for m in range(8)]) for l in L]))
    w["fingT"] = fm(inp["final_g"], 8)
    return w


def blocks_of(include_ctx=True):
    bl = []
    if include_ctx:
        bl.append((0, CTX, CTX))
    for i in range(SEQ // NB):
        bl.append((CTX + i * NB, NB, 64))
    return bl


class Builder:
    def __init__(self, debug=None, stop_after=None, layers=DEPTH, max_blocks=None, needed=None):
        self.max_blocks = max_blocks
        self.needed = needed
        self.debug = debug or []
        self.stop_after = stop_after
        self.layers = layers
        self.nc = bass.Bass("TRN2", target_bir_lowering=False)
        nc = self.nc
        self.inp = {}
        self.inp["xT0"] = nc.dram_tensor("xT0", [D, T], F32, kind="ExternalInput").ap()
        self.inp["ccT"] = nc.dram_tensor("ccT", [128, 8, 2], F32, kind="ExternalInput").ap()
        for k, shp in CONST_SHAPES.items():
            self.inp[k] = nc.dram_tensor(k, list(shp), F32, kind="ExternalInput").ap()
        for k, shp in WEIGHT_SHAPES.items():
            self.inp[k] = nc.dram_tensor(k, list(shp), F32, kind="ExternalInput").ap()
        self.outT = nc.dram_tensor("outT", [D, SEQ], F32, kind="ExternalOutput").ap()
        self.scr = {}
        for name, shp in dict(
            qT=[4, 128, T], kT=[4, 128, T], ktok=[T, 4, 128], vtok=[T, 4, 128], zT=[4, 128, T], fnT=[4, 128, T],
            m1T=[8, 128, T], g0T=[8, 128, T], g2T=[8, 128, T], bg=[T, 16], otok=[T, 4, 128], otokb=[T, 4, 128], ydnT=[4, 128, T],
            YT=[4, 128, T], Ap=[4, 2 * (SEQ // 128), 128, 128], xT1=[D, T],
        ).items():
            kind = "ExternalOutput" if name in self.debug else "Internal"
            self.scr[name] = nc.dram_tensor("s_" + name, shp, F32, kind=kind).ap()
        if MMT == BF16:
            for name, shp in dict(wb_dn=[128, 8, 512], wb_fn=[128, 8, 512], wb_o=[2, 128, 8, 512],
                                  wb_f1=[FC // 2, 128, 8, 512], wb_f2=[8, 128, FC, 128]).items():
                self.scr[name] = nc.dram_tensor("s_" + name, shp, BF16, kind="Internal").ap()

    def build(self):
        with ExitStack() as st:
            self.c = Ctx(self.nc, st, needed=self.needed)
            self._consts(st)
            try:
                for l in range(self.layers):
                    self.layer(l)
            except StopIteration:
                pass
            self.c.finish()
        return self.nc

    def dbg(self, name, tv, shape):
        if name not in self.debug:
            return
        d = self.nc.dram_tensor("dbg_" + name, list(shape), F32, kind="ExternalOutput").ap()
        self.c.store(d, tv)

    def _stop(self, tag):
        if self.stop_after == tag:
            raise StopIteration

    def _consts(self, st):
        c = self.c
        self.K = {}
        for k in ("ident", "ones", "Uf", "Ub", "NSf", "NSb"):
            t = c.sbuf([128, 128], name="k_" + k)
            c.load(t.full(), self.inp[k])
            self.K[k] = t
        for k in ("NIf", "NIb"):
            t = c.sbuf([64, 64], name="k_" + k)
            c.load(t.full(), self.inp[k])
            self.K[k] = t
        t = c.sbuf([128, 2, 128], name="k_SelC")
        c.load(t.full(), self.inp["SelC"])
        self.K["SelC"] = t
        self.K["eps"] = c.sbuf([128, 1], name="k_eps")
        c.memset(self.K["eps"].full(), EPS)
        self.K["negones"] = c.sbuf([128, 128], name="k_negones")
        c.memset(self.K["negones"].full(), -1.0)
        self.cc = c.sbuf([128, 8, 2], name="cc")
        self.modv = [dict() for _ in range(DEPTH)]
        for l in range(DEPTH):
            for nm in ("A1", "B1", "G1", "A2", "B2", "G2"):
                self.modv[l][nm] = c.sbuf([128, 8, 2], name=f"{nm}_{l}")
        self.PS = [c.psum([128, 512], name=f"ps{i}") for i in range(8)]
        self.psi = 0

    def ps(self):
        p = self.PS[self.psi % 8]
        self.psi += 1
        return p

    def layer(self, l):
        self.phase_mod(l)
        self._stop(f"mod{l}")
        self.phase_p1(l)
        self._stop(f"p1_{l}")
        self.phase_fnet(l)
        self._stop(f"p2_{l}")
        self.phase_dn(l)
        self._stop(f"p3_{l}")
        self.phase_p4(l)
        self._stop(f"p4_{l}")

    def phase_mod(self, l):
        c, K = self.c, self.K
        if l == 0:
            c.load(self.cc.full(), self.inp["ccT"])
            c.act_fn(self.cc.full(), self.cc.full(), AF.Silu)
        with ExitStack() as ph:
            modT = c.sbuf([128, 48, 2], name=f"modT{l}", stack=ph)
            bmod = c.sbuf([128, 48], name=f"bmod{l}", stack=ph)
            c.load(bmod.full(), self.inp["bmodT"][l])
            wts = [c.sbuf([128, 8, 512], name=f"wmod{i}", stack=ph) for i in range(2)]
            wsrc = self.inp["w_mod"][l].rearrange("(kc p) f -> p kc f", p=128)
            for gi in range(12):
                wt = wts[gi % 2]
                c.load(wt.full(), wsrc[:, :, gi * 512:(gi + 1) * 512])
                p = self.ps()
                for j in range(4):
                    for kc in range(8):
                        c.mm(p[:, j * 2:(j + 1) * 2], wt[:, kc, j * 128:(j + 1) * 128], self.cc[:, kc, :],
                             start=(kc == 0), stop=(kc == 7))
                for j in range(4):
                    fcx = gi * 4 + j
                    c.ts(modT[:, fcx, :], p[:, j * 2:(j + 1) * 2], bmod[:, fcx:fcx + 1], ALU.add)
            mv = self.modv[l]
            n1g = c.sbuf([128, 8], name=f"n1g{l}", stack=ph)
            n2g = c.sbuf([128, 8], name=f"n2g{l}", stack=ph)
            c.load(n1g.full(), self.inp["n1gT"][l])
            c.load(n2g.full(), self.inp["n2gT"][l])
            for col in range(2):
                c.copy(mv["B1"][:, :, col], modT[:, 0:8, col])
                c.stt(mv["A1"][:, :, col], modT[:, 8:16, col], 1.0, n1g.full(), ALU.add, ALU.mult)
                c.copy(mv["G1"][:, :, col], modT[:, 16:24, col])
                c.copy(mv["B2"][:, :, col], modT[:, 24:32, col])
                c.stt(mv["A2"][:, :, col], modT[:, 32:40, col], 1.0, n2g.full(), ALU.add, ALU.mult)
                c.copy(mv["G2"][:, :, col], modT[:, 40:48, col])
            self.dbg(f"modT{l}", modT.full(), [128, 48, 2])
            self.dbg(f"A1_{l}", mv["A1"].full(), [128, 8, 2])
            c.barrier()

    def run_units(self, units, nslots):
        active = []
        free = list(range(nslots))
        it = iter(units)
        pending = None
        done = False
        while True:
            while free and not done:
                u = pending if pending is not None else next(it, None)
                pending = None
                if u is None:
                    done = True
                    break
                if isinstance(u, str):
                    if active:
                        pending = u
                        break
                    continue
                sl = free.pop(0)
                active.append((sl, u(sl)))
            if not active:
                if done:
                    break
                continue
            for ent in list(active):
                try:
                    next(ent[1])
                except StopIteration:
                    active.remove(ent)
                    free.append(ent[0])
                    free.sort()

    def norm_mod(self, X, H, A, Bv, col, w, ph_tiles):
        c, K = self.c, self.K
        sq0, sq1, rstd = ph_tiles
        p = self.ps()
        for kc in range(KC):
            sq = sq0 if kc % 2 == 0 else sq1
            if kc % 2 == 0:
                c.act_fn(sq[:, :w], X[:, kc, :w], AF.Square)
            else:
                c.tt(sq[:, :w], X[:, kc, :w], X[:, kc, :w], ALU.mult)
            c.mm(p[:, :w], K["ones"].full(), sq[:, :w], start=(kc == 0), stop=(kc == KC - 1))
        c.act_fn(rstd[:, :w], p[:, :w], AF.Sqrt, bias=K["eps"].full(), scale=1.0 / D)
        c.recip(rstd[:, :w], rstd[:, :w])
        for kc in range(KC):
            tmp_ = sq0 if kc % 2 == 0 else sq1
            c.tt(tmp_[:, :w], X[:, kc, :w], rstd[:, :w], ALU.mult)
            c.ts(H[:, kc, :w].r(), tmp_[:, :w], A[:, kc, col:col + 1], ALU.mult, Bv[:, kc, col:col + 1], ALU.add)

    def conv3(self, out, x, wk, w, rowlen):
        c = self.c
        xv = x[:, :w].re("p (r t) -> p r t", t=rowlen)
        ov = out[:, :w].re("p (r t) -> p r t", t=rowlen)
        c.ts(out[:, :w], x[:, :w], wk[:, 1:2], ALU.mult)
        c.stt(ov[:, :, 1:rowlen], xv[:, :, 0:rowlen - 1], wk[:, 0:1], ov[:, :, 1:rowlen], ALU.mult, ALU.add)
        c.stt(ov[:, :, 0:rowlen - 1], xv[:, :, 1:rowlen], wk[:, 2:3], ov[:, :, 0:rowlen - 1], ALU.mult, ALU.add)

    def phase_p1(self, l):
        c, K, S = self.c, self.K, self.scr
        mv = self.modv[l]
        xsrc = (self.inp["xT0"] if l == 0 else S["xT1"]).rearrange("(kc p) t -> p kc t", p=128)
        win = self.inp["w_in_t"][l]
        wgate = self.inp["w_gate_t"][l]
        with ExitStack() as ph:
            X = c.sbuf([128, KC, NB], name="p1X", stack=ph)
            H = c.sbuf([128, KC, NB], dtype=MMT, name="p1H", stack=ph)
            RES = (MMT == BF16)
            NW1 = 14 if RES else 4
            W = [c.sbuf([128, KC, 512], dtype=MMT, name=f"p1W{i}", stack=ph) for i in range(NW1)]
            wi = [0]
            Wres = {}
            if RES:
                for gi in range(8):
                    Wres[("in", gi)] = W[gi]
                    c.load_r(W[gi].full(), win[gi])
                for g6 in range(6):
                    Wres[("g", g6 // 2, g6 % 2)] = W[8 + g6]
                    c.load_r(W[8 + g6].full(), wgate[g6])
            Wba = c.sbuf([128, KC, 16], dtype=MMT, name="p1Wba", stack=ph)
            (c.load_r if MMT != F32 else c.load)(Wba.full(), self.inp["w_ba"][l])
            Wsc = c.sbuf([128, 4, D], dtype=MMT, name="p1Wsc", stack=ph)
            c.load_r(Wsc.full(), self.inp["w_sc_t"][l])
            dncw = c.sbuf([128, 12, 3], name="p1dncw", stack=ph)
            c.load(dncw.full(), self.inp["dncwT"][l])
            sccw = c.sbuf([128, 4, 3], name="p1sccw", stack=ph)
            c.load(sccw.full(), self.inp["sccwT"][l])
            bgate = c.sbuf([128, 24], name="p1bgate", stack=ph)
            c.load(bgate.full(), self.inp["bgateT"][l])
            nalog = c.sbuf([128, 8], name="p1nalog", stack=ph)
            c.load(nalog.full(), self.inp["dn_neg_alog"][l])
            c.act_fn(nalog.full(), nalog.full(), AF.Exp)
            c.ts(nalog.full(), nalog.full(), -1.0, ALU.mult)
            dtb = c.sbuf([128, 8], name="p1dtb", stack=ph)
            c.load(dtb.full(), self.inp["dn_dtb"][l])
            xa = [c.sbuf([128, NB], name=f"p1xa{i}", stack=ph) for i in range(4)]
            xb = [c.sbuf([128, NB], name=f"p1xb{i}", stack=ph) for i in range(4)]
            stq = [c.sbuf([128, NB], name=f"p1st{i}", stack=ph) for i in range(4)]
            PSs = self.PS
            kfm = [c.sbuf([128, NB], name=f"p1kfm{j}", stack=ph) for j in range(4)]
            scin = kfm
            scf = [c.sbuf([128, NB], dtype=MMT, name=f"p1scf{j}", stack=ph) for j in range(4)]
            tok = [c.sbuf([128, 4, 128], name=f"p1tok{i}", stack=ph) for i in range(4)]
            vfm = [c.sbuf([128, NB], name=f"p1vfm{j}", stack=ph) for j in range(4)]
            bgt = c.sbuf([128, 4, 16], name="p1bg", stack=ph)
            bgx = c.sbuf([128, 4, 8], name="p1bgx", stack=ph)

            def next_w(src_ap):
                wt = W[wi[0] % NW1]
                wi[0] += 1
                c.load_r(wt.full(), src_ap)
                return wt

            def proj(wt, j, w, p=None):
                p = p or self.ps()
                for kc in range(KC):
                    c.mm(p[:, :w], wt[:, kc, j * 128:(j + 1) * 128], H[:, kc, :w], start=(kc == 0), stop=(kc == KC - 1), r=True)
                return p

            def stage():
                t = st_t[sti[0] % 4]
                sti[0] += 1
                return t

            nblk = 0
            for (t0, w, rowlen) in blocks_of(True):
                col = 1 if t0 == 0 else 0
                c.load(X[:, :, :w], xsrc[:, :, t0:t0 + w], dram_r="xT1")
                self.norm_mod(X, H, mv["A1"], mv["B1"], col, w, (xa[0], xa[1], xb[0]))
                if nblk == 0:
                    self.dbg(f"X{l}", X.full(), [128, KC, NB])
                ntt = w // 128
                wcache = {}

                def getw(key, src_ap):
                    if RES:
                        return Wres[key]
                    if key not in wcache:
                        wcache[key] = next_w(src_ap)
                    return wcache[key]

                def conv3_g(out, x, wk):
                    xv = x[:, :w].re("p (r t) -> p r t", t=rowlen)
                    ov = out[:, :w].re("p (r t) -> p r t", t=rowlen)
                    c.ts(out[:, :w], x[:, :w], wk[:, 1:2], ALU.mult)
                    yield
                    c.stt(ov[:, :, 1:rowlen], xv[:, :, 0:rowlen - 1], wk[:, 0:1], ov[:, :, 1:rowlen], ALU.mult, ALU.add)
                    yield
                    c.stt(ov[:, :, 0:rowlen - 1], xv[:, :, 1:rowlen], wk[:, 2:3], ov[:, :, 0:rowlen - 1], ALU.mult, ALU.add)
                    yield

                def u_qk(gi, j):
                    nm, scale = (("qT", 128.0 ** -0.5), ("kT", 1.0))[gi]

                    def g(sl):
                        wt = getw(("in", gi), win[gi])
                        p, p2, a_, b2 = PSs[2 * sl], PSs[2 * sl + 1], xa[sl], xb[sl]
                        d2 = kfm[j] if gi == 1 else stq[sl]
                        proj(wt, j, w, p)
                        yield
                        c.copy(a_[:, :w], p[:, :w], E=c.act)
                        yield
                        yield from conv3_g(b2, a_, dncw[:, gi * 4 + j, :])
                        c.act_fn(b2[:, :w], b2[:, :w], AF.Silu)
                        yield
                        c.tt(a_[:, :w], b2[:, :w], b2[:, :w], ALU.mult)
                        yield
                        c.mm(p2[:, :w], K["ones"].full(), a_[:, :w])
                        yield
                        c.act_fn(a_[:, :w], p2[:, :w], AF.Sqrt, bias=K["eps"].full(), scale=1.0)
                        yield
                        c.recip(a_[:, :w], a_[:, :w])
                        yield
                        c.stt(d2[:, :w], b2[:, :w], scale, a_[:, :w], ALU.mult, ALU.mult)
                        c.store(S[nm][j, :, t0:t0 + w], d2[:, :w], dram_w=nm)
                    return g

                def u_tok(src, dst_name, tt_):
                    def g(sl):
                        tk = tok[sl]
                        for j in range(4):
                            p3 = PSs[2 * sl + (j % 2)]
                            c.transpose(p3[:, 0:128], src[j][:, tt_ * 128:(tt_ + 1) * 128], K["ident"].full())
                            yield
                            c.copy(tk[:, j, :], p3[:, 0:128], E=(c.act if j % 2 else c.dve))
                            yield
                        c.store(S[dst_name][t0 + tt_ * 128:t0 + (tt_ + 1) * 128, :, :], tk.full(), dram_w=dst_name)
                    return g

                def u_v(j):
                    def g(sl):
                        wt = getw(("in", 2), win[2])
                        p, a_ = PSs[2 * sl], xa[sl]
                        proj(wt, j, w, p)
                        yield
                        c.copy(a_[:, :w], p[:, :w], E=c.act)
                        yield
                        yield from conv3_g(vfm[j], a_, dncw[:, 8 + j, :])
                        c.act_fn(vfm[j][:, :w], vfm[j][:, :w], AF.Silu)
                    return g

                def u_z(j):
                    def g(sl):
                        wt = getw(("in", 3), win[3])
                        p, d2 = PSs[2 * sl], stq[sl]
                        proj(wt, j, w, p)
                        yield
                        c.act_fn(d2[:, :w], p[:, :w], AF.Silu)
                        c.store(S["zT"][j, :, t0:t0 + w], d2[:, :w], dram_w="zT")
                    return g

                def u_ba():
                    def g(sl):
                        p = PSs[2 * sl]
                        for tt_ in range(ntt):
                            for kc in range(KC):
                                c.mm(p[:, tt_ * 16:(tt_ + 1) * 16], H[:, kc, tt_ * 128:(tt_ + 1) * 128], Wba[:, kc, :],
                                     start=(kc == 0), stop=(kc == KC - 1))
                        yield
                        pv = p[:, 0:ntt * 16].re("p (t f) -> p t f", f=16)
                        c.act_fn(bgt[:, :ntt, 0:8], pv[:, :, 0:8], AF.Sigmoid)
                        yield
                        for tt_ in range(ntt):
                            c.tt(bgx[:, tt_, :], pv[:, tt_, 8:16], dtb.full(), ALU.add)
                        yield
                        c.act_fn(bgx[:, :ntt, :], bgx[:, :ntt, :], AF.Exp)
                        yield
                        c.act_fn(bgx[:, :ntt, :], bgx[:, :ntt, :], AF.Ln, bias=1.0)
                        yield
                        for tt_ in range(ntt):
                            c.tt(bgt[:, tt_, 8:16], bgx[:, tt_, :], nalog.full(), ALU.mult)
                        c.store(S["bg"][t0:t0 + w, :].rearrange("(t p) f -> p t f", p=128), bgt[:, :ntt, :], dram_w="bg")
                    return g

                def u_scC(j):
                    def g(sl):
                        wt = getw(("in", 5), win[5])
                        p = PSs[2 * sl]
                        proj(wt, j, w, p)
                        yield
                        c.copy(scin[j][:, :w], p[:, :w], E=c.act)
                    return g

                def u_scX(j):
                    def g(sl):
                        wt = getw(("in", 6), win[6])
                        p, a_ = PSs[2 * sl], xa[sl]
                        proj(wt, j, w, p)
                        yield
                        c.tt(a_[:, :w], p[:, :w], scin[j][:, :w], ALU.mult)
                        yield
                        yield from conv3_g(scin[j], a_, sccw[:, j, :])
                    return g

                def u_scB(j):
                    def g(sl):
                        wt = getw(("in", 4), win[4])
                        p = PSs[2 * sl]
                        proj(wt, j, w, p)
                        yield
                        c.tt(scf[j][:, :w].r(), p[:, :w], scin[j][:, :w], ALU.mult)
                    return g

                def u_gate(gidx, nm, mc):
                    def g(sl):
                        half, j = mc // 4, mc % 4
                        wt = getw(("g", gidx, half), wgate[gidx * 2 + half])
                        p, p2, d2 = PSs[2 * sl], PSs[2 * sl + 1], stq[sl]
                        proj(wt, j, w, p)
                        yield
                        if gidx == 1:
                            for kc in range(4):
                                c.mm(p2[:, :w], Wsc[:, kc, mc * 128:(mc + 1) * 128], scf[kc][:, :w],
                                     start=(kc == 0), stop=(kc == 3), r=True)
                            yield
                        c.act_fn(d2[:, :w], p[:, :w], AF.Sigmoid, bias=bgate[:, gidx * 8 + mc: gidx * 8 + mc + 1])
                        yield
                        if gidx == 1:
                            c.tt(d2[:, :w], d2[:, :w], p2[:, :w], ALU.mult)
                        c.store(S[nm][mc, :, t0:t0 + w], d2[:, :w], dram_w=nm)
                    return g

                def u_fn(j):
                    def g(sl):
                        wt = getw(("in", 7), win[7])
                        p, d2 = PSs[2 * sl], stq[sl]
                        proj(wt, j, w, p)
                        yield
                        c.copy(d2[:, :w], p[:, :w], E=(c.act if j % 2 else c.dve))
                        c.store(S["fnT"][j, :, t0:t0 + w], d2[:, :w], dram_w="fnT")
                    return g

                FL = "flush"
                units = [u_qk(0, j) for j in range(4)] + [u_qk(1, j) for j in range(4)] + [FL]
                units += [u_tok(kfm, "ktok", tt_) for tt_ in range(ntt)] + [u_v(j) for j in range(4)] + [FL]
                units += [u_tok(vfm, "vtok", tt_) for tt_ in range(ntt)] + [u_z(j) for j in range(4)] + [u_ba()]
                units += [u_scC(j) for j in range(4)] + [FL] + [u_scX(j) for j in range(4)] + [FL]
                units += [u_scB(j) for j in range(4)] + [FL]
                units += [u_gate(1, "m1T", mc) for mc in range(8)] + [u_gate(0, "g0T", mc) for mc in range(8)]
                units += [u_gate(2, "g2T", mc) for mc in range(8)] + [u_fn(j) for j in range(4)]
                self.run_units(units, 4)
                nblk += 1
                if self.max_blocks and nblk >= self.max_blocks:
                    break
            c.barrier()


    def phase_fnet(self, l):
        c, K, S = self.c, self.K, self.scr
        N2 = SEQ // 128
        with ExitStack() as ph:
            R0 = c.sbuf([128, 512], name="fR0", stack=ph)
            c.load(R0.full(), self.inp["R0"])
            C128 = c.sbuf([128, 128], name="fC128", stack=ph)
            c.load(C128.full(), self.inp["C128"])
            S128 = c.sbuf([128, 128], name="fS128", stack=ph)
            c.load(S128.full(), self.inp["S128"])
            tw = {}
            for nm in ("twr", "twi", "ntwi"):
                tw[nm] = c.sbuf([128, N2], name="f" + nm, stack=ph)
                c.load(tw[nm].full(), self.inp[nm])
            G = c.sbuf([2 * N2, N2], name="fG", stack=ph)
            c.load(G.full(), self.inp["G64"])
            UT = c.sbuf([128, SEQ], name="fUT", stack=ph)
            YTt = c.sbuf([128, SEQ], name="fYT", stack=ph)
            A2 = c.sbuf([2 * N2, 128, 128], name="fA2", stack=ph)
            Zs = [c.sbuf([128, 512], name=f"fZs{i}", stack=ph) for i in range(2)]
            Apt = [c.sbuf([128, 2, 128], name=f"fAp{i}", stack=ph) for i in range(2)]
            t12 = [c.sbuf([128, 2, 128], name=f"ft12{i}", stack=ph) for i in range(2)]
            for g in range(4):
                c.load(UT.full(), S["fnT"][g, :, CTX:T], dram_r="fnT")
                UTv = UT.full().re("p (n1 n2) -> p n1 n2", n2=N2)
                apd = S["Ap"][g].rearrange("(ri n2) k c -> k ri n2 c", ri=2)
                for n2 in range(N2):
                    pa = self.ps()
                    c.mm(pa.full(), UTv[:, :, n2], R0.full())
                    zs = Zs[n2 % 2]
                    c.copy(zs.full(), pa.full(), E=c.act)
                    pb = self.ps()
                    c.mm(pb[:, 0:256], C128.full(), zs[:, 0:256], start=True, stop=False)
                    c.mm(pb[:, 0:256], S128.full(), zs[:, 256:512], start=False, stop=True)
                    ap_, tt_ = Apt[n2 % 2], t12[n2 % 2]
                    c.ts(tt_[:, 0, :], pb[:, 0:128], tw["twr"][:, n2:n2 + 1], ALU.mult)
                    c.ts(tt_[:, 1, :], pb[:, 0:128], tw["twi"][:, n2:n2 + 1], ALU.mult)
                    c.stt(ap_[:, 0, :], pb[:, 128:256], tw["ntwi"][:, n2:n2 + 1], tt_[:, 0, :], ALU.mult, ALU.add)
                    c.stt(ap_[:, 1, :], pb[:, 128:256], tw["twr"][:, n2:n2 + 1], tt_[:, 1, :], ALU.mult, ALU.add)
                    c.store(apd[:, :, n2, :], ap_.full(), dram_w="Ap")
                c.load(A2.full(), S["Ap"][g], dram_r="Ap")
                YTv = YTt.full().re("p (k2 k1) -> p k1 k2", k1=128)
                kb = max(1, 512 // N2)
                kb = min(kb, 128)
                for k1b in range(128 // kb):
                    pc = self.ps()
                    for i in range(kb):
                        k1 = k1b * kb + i
                        c.mm(pc[:, i * N2:(i + 1) * N2], A2[:, k1, :], G.full())
                    c.copy(YTv[:, k1b * kb:(k1b + 1) * kb, :], pc[:, 0:kb * N2].re("p (k1 k2) -> p k1 k2", k2=N2),
                           E=(c.act if k1b % 2 else c.dve))
                c.store(S["YT"][g, :, CTX:T], YTt.full(), dram_w="YT")
            c.barrier()
        if l == 0:
            with ExitStack() as ph:
                R0c = c.sbuf([128, 256], name="fR0c", stack=ph)
                c.load(R0c.full(), self.inp["R0c"])
                C256 = c.sbuf([128, 2, 256], name="fC256", stack=ph)
                c.load(C256.full(), self.inp["C256"].rearrange("(a p) k -> p a k", p=128))
                S256 = c.sbuf([128, 2, 256], name="fS256", stack=ph)
                c.load(S256.full(), self.inp["S256"].rearrange("(a p) k -> p a k", p=128))
                UTc = c.sbuf([128, 256], name="fUTc", stack=ph)
                Zc = [c.sbuf([128, 256], name=f"fZc{i}", stack=ph) for i in range(2)]
                Yc = c.sbuf([128, 256], name="fYc", stack=ph)
                for g in range(4):
                    c.load(UTc.full(), S["fnT"][g, :, 0:CTX], dram_r="fnT")
                    for tt_ in range(2):
                        pa = self.ps()
                        c.mm(pa[:, 0:256], UTc[:, tt_ * 128:(tt_ + 1) * 128], R0c.full())
                        c.copy(Zc[tt_].full(), pa[:, 0:256], E=c.act)
                    py = self.ps()
                    for tt_ in range(2):
                        c.mm(py[:, 0:256], Zc[tt_][:, 0:128], C256[:, tt_, :], start=(tt_ == 0), stop=False)
                        c.mm(py[:, 0:256], Zc[tt_][:, 128:256], S256[:, tt_, :], start=False, stop=(tt_ == 1))
                    c.copy(Yc.full(), py[:, 0:256])
                    c.store(S["YT"][g, :, 0:CTX], Yc.full(), dram_w="YT")
                c.barrier()

    def phase_dn(self, l):
        c, K, S = self.c, self.K, self.scr
        nblk = T // 128
        orders = [list(range(nblk)), [1, 0] + list(range(nblk - 1, 1, -1))]
        pos = [{b: i for i, b in enumerate(o)} for o in orders]
        corders = [(0, 1), (1, 0)]
        Us = [K["Uf"], K["Ub"]]
        NSs = [K["NSf"], K["NSb"]]
        NIs = [K["NIf"], K["NIb"]]
        kTv = S["kT"].rearrange("h p t -> p h t")
        qTv = S["qT"].rearrange("h p t -> p h t")
        zTv = S["zT"].rearrange("h p t -> p h t")
        yTv = S["ydnT"].rearrange("h p t -> p h t")
        otk = [S["otok"], S["otokb"]]
        otn = ["otok", "otokb"]
        HD = [(dr, h) for dr in range(2) for h in range(4)]
        with ExitStack() as ph:
            sb = lambda shp, nm: c.sbuf(shp, name=f"d{nm}", stack=ph)
            D2 = lambda f: [f(dr) for dr in range(2)]
            St = D2(lambda dr: [sb([128, 128], f"S{dr}{h}") for h in range(4)])
            for dr, h in HD:
                c.memset(St[dr][h].full(), 0.0)
            onorm = sb([128, 1], "onorm")
            c.load(onorm.full(), self.inp["onormT"][l])
            kT4 = D2(lambda dr: [sb([128, 4, 128], f"kT4{dr}{i}") for i in range(2)])
            qT4 = D2(lambda dr: [sb([128, 4, 128], f"qT4{dr}{i}") for i in range(2)])
            ktb = D2(lambda dr: [sb([128, 4, 128], f"ktb{dr}{i}") for i in range(2)])
            vtb = D2(lambda dr: [sb([128, 4, 128], f"vtb{dr}{i}") for i in range(2)])
            kt1 = D2(lambda dr: [sb([64, 4, 128], f"kt1{dr}{i}") for i in range(2)])
            bgb = D2(lambda dr: [sb([128, 16], f"bgb{dr}{i}") for i in range(2)])
            zb = D2(lambda dr: sb([128, 4, 128], f"zb{dr}"))
            ofl = D2(lambda dr: [sb([64, 4, 128], f"of{dr}{cc}") for cc in range(2)])
            sm = D2(lambda dr: sb([128, 20], f"sm{dr}"))
            esm = D2(lambda dr: sb([128, 20], f"esm{dr}"))
            nsm = D2(lambda dr: sb([128, 20], f"nsm{dr}"))
            kds = D2(lambda dr: sb([64, 8], f"kds{dr}"))
            bexp = D2(lambda dr: sb([128, 4], f"bexp{dr}"))
            nbeta = D2(lambda dr: sb([128, 4], f"nbeta{dr}"))
            rhsG = D2(lambda dr: [sb([128, 128], f"rhsG{dr}{h}") for h in range(4)])
            Dst = D2(lambda dr: [sb([128, 128], f"Dst{dr}{h}") for h in range(4)])
            Nt = D2(lambda dr: [[sb([128, 128], f"N{dr}{h}{i}") for i in range(2)] for h in range(4)])
            MS = D2(lambda dr: [[sb([128, 256], f"MS{dr}{h}{i}") for i in range(2)] for h in range(4)])
            TT = D2(lambda dr: [sb([128, 128], f"TT{dr}{h}") for h in range(4)])
            Rv = D2(lambda dr: [sb([128, 128], f"Rv{dr}{h}") for h in range(4)])
            Rw = D2(lambda dr: [sb([128, 128], f"Rw{dr}{h}") for h in range(4)])
            ut = D2(lambda dr: [[sb([64, 128], f"u{dr}{h}{cc}") for cc in range(2)] for h in range(4)])
            wT = D2(lambda dr: [sb([128, 128], f"wT{dr}{h}") for h in range(4)])
            DT = D2(lambda dr: [[sb([64, 64], f"DT{dr}{h}{cc}") for cc in range(2)] for h in range(4)])
            aT = D2(lambda dr: [[sb([64, 64], f"aT{dr}{h}{cc}") for cc in range(2)] for h in range(4)])
            kd = D2(lambda dr: [[sb([64, 128], f"kd{dr}{h}{cc}") for cc in range(2)] for h in range(4)])
            vn = D2(lambda dr: [sb([64, 128], f"vn{dr}{h}") for h in range(4)])
            o1 = D2(lambda dr: [sb([64, 128], f"o1{dr}{h}") for h in range(4)])
            ob = D2(lambda dr: [[sb([64, 4, 128], f"ob{dr}{i}{cc}") for cc in range(2)] for i in range(2)])
            osq = sb([64, 4, 128], "osq")
            oss = sb([64, 4], "oss")
            yT = D2(lambda dr: sb([128, 4, 128], f"yT{dr}"))
            ident, ones, negones = K["ident"], K["ones"], K["negones"]
            BK = self.PS
            import os
            for it in range(min(nblk, int(os.environ.get("DN_ITERS", "100000")))):
                par = it % 2
                blk = [orders[0][it], orders[1][it]]
                comb = [pos[1 - dr][blk[dr]] < it for dr in range(2)]
                k4, q4, kb_, vb_, k1_, bg_, beta, gD = ([None, None] for _ in range(8))
                for dr in range(2):
                    t0 = blk[dr] * 128
                    k4[dr], q4[dr], kb_[dr], vb_[dr], k1_[dr], bg_[dr] = (kT4[dr][par], qT4[dr][par], ktb[dr][par],
                                                                          vtb[dr][par], kt1[dr][par], bgb[dr][par])
                    c.load(k4[dr].full(), kTv[:, :, t0:t0 + 128], dram_r="kT")
                    c.load(q4[dr].full(), qTv[:, :, t0:t0 + 128], dram_r="qT")
                    c.load(kb_[dr].full(), S["ktok"][t0:t0 + 128], dram_r="ktok")
                    c.load(vb_[dr].full(), S["vtok"][t0:t0 + 128], dram_r="vtok")
                    c.load(k1_[dr].full(), S["ktok"][t0 + 64:t0 + 128], dram_r="ktok")
                    c.load(bg_[dr].full(), S["bg"][t0:t0 + 128, :], dram_r="bg")
                    if comb[dr]:
                        c.load(zb[dr].full(), zTv[:, :, t0:t0 + 128], dram_r="zT")
                        for cc in range(2):
                            c.load(ofl[dr][cc].full(), otk[1 - dr][t0 + cc * 64:t0 + (cc + 1) * 64], dram_r=otn[1 - dr])
                    beta[dr] = bg_[dr][:, dr * 4:(dr + 1) * 4]
                    gD[dr] = bg_[dr][:, 8 + dr * 4:8 + (dr + 1) * 4]
                for dr in range(2):
                    U = Us[dr]
                    p = self.ps()
                    c.mm(p[:, 0:4], U.full(), gD[dr])
                    for cc in range(2):
                        c.mm(p[0:64, 4 + cc * 4:8 + cc * 4], U[:, cc * 64:(cc + 1) * 64], gD[dr])
                        c.mm(p[:, 12 + cc * 4:16 + cc * 4], K["SelC"][:, cc, :], gD[dr])
                    c.copy(sm[dr].full(), p[:, 0:20])
                    c.act_fn(esm[dr].full(), p[:, 0:20], AF.Exp)
                    c.ts(nsm[dr].full(), sm[dr].full(), -1.0, ALU.mult)
                    for cc in range(2):
                        c.tt(kds[dr][:, cc * 4:(cc + 1) * 4], sm[dr][0:64, 12 + cc * 4:16 + cc * 4],
                             sm[dr][0:64, 4 + cc * 4:8 + cc * 4], ALU.subtract)
                    c.act_fn(kds[dr].full(), kds[dr].full(), AF.Exp)
                    c.tt(bexp[dr].full(), beta[dr], esm[dr][:, 0:4], ALU.mult)
                    c.ts(nbeta[dr].full(), beta[dr], -1.0, ALU.mult)
                for i, (dr, h) in enumerate(HD):
                    bk = BK[i]
                    c.mm(bk[:, 0:128], k4[dr][:, h, :], k4[dr][:, h, :])
                    c.ts(rhsG[dr][h].full(), Us[dr].full(), gD[dr][:, h:h + 1], ALU.mult)
                    c.mm(bk[:, 128:256], negones.full(), rhsG[dr][h].full(), start=True, stop=False)
                    c.mm(bk[:, 128:256], ident.full(), NSs[dr].full(), start=False, stop=True)
                for i, (dr, h) in enumerate(HD):
                    bk = BK[i]
                    c.act_fn(Dst[dr][h].full(), bk[:, 128:256], AF.Exp, bias=sm[dr][:, h:h + 1])
                    c.stt(Nt[dr][h][0].full(), bk[:, 0:128], nbeta[dr][:, h:h + 1], Dst[dr][h].full(), ALU.mult, ALU.mult)
                for i, (dr, h) in enumerate(HD):
                    bk = BK[i]
                    c.transpose(bk[:, 0:128], Nt[dr][h][0].full(), ident.full())
                for i, (dr, h) in enumerate(HD):
                    bk = BK[i]
                    c.copy(MS[dr][h][0][:, 0:128], bk[:, 0:128], E=c.act)
                    c.tt(MS[dr][h][0][:, 128:256], bk[:, 0:128], ident.full(), ALU.add)
                for k in range(6):
                    a, bn = k % 2, (k + 1) % 2
                    for dr in range(2):
                        pms, pns = [], []
                        for h in range(4):
                            N_, MS_ = Nt[dr][h][a], MS[dr][h][a]
                            pm = self.ps()
                            if k == 0:
                                c.mm(pm[:, 0:128], N_.full(), MS_[:, 0:128])
                            elif k <= 3:
                                c.mm(pm[:, 0:256], N_.full(), MS_.full())
                            else:
                                c.mm(pm[:, 0:128], N_.full(), MS_[:, 128:256])
                            pms.append(pm)
                            if k <= 4:
                                pn = self.ps()
                                c.mm(pn[:, 0:128], MS_[:, 0:128], N_.full())
                                pns.append(pn)
                        for h in range(4):
                            MSa, MSb = MS[dr][h][a], MS[dr][h][bn]
                            if k == 0:
                                c.copy(MSb[:, 0:128], pms[h][:, 0:128], E=c.act)
                                c.copy(MSb[:, 128:256], MSa[:, 128:256])
                            elif k <= 3:
                                c.copy(MSb[:, 0:128], pms[h][:, 0:128], E=c.act)
                                c.tt(MSb[:, 128:256], pms[h][:, 128:256], MSa[:, 128:256], ALU.add)
                            elif k == 4:
                                c.tt(MSb[:, 128:256], pms[h][:, 0:128], MSa[:, 128:256], ALU.add)
                            else:
                                c.tt(TT[dr][h].full(), pms[h][:, 0:128], MSa[:, 128:256], ALU.add)
                            if k <= 4:
                                c.copy(Nt[dr][h][bn].full(), pns[h][:, 0:128], E=c.act)
                for dr in range(2):
                    for h in range(4):
                        c.ts(Rv[dr][h].full(), vb_[dr][:, h, :], beta[dr][:, h:h + 1], ALU.mult)
                        c.ts(Rw[dr][h].full(), kb_[dr][:, h, :], bexp[dr][:, h:h + 1], ALU.mult)
                        for cc in range(2):
                            pu = self.ps()
                            c.mm(pu[0:64, 0:128], TT[dr][h][:, cc * 64:(cc + 1) * 64], Rv[dr][h].full())
                            c.copy(ut[dr][h][cc].full(), pu[0:64, 0:128], E=c.act)
                        pw = self.ps()
                        c.mm(pw[:, 0:128], Rw[dr][h].full(), TT[dr][h].full())
                        c.copy(wT[dr][h].full(), pw[:, 0:128], E=c.act)
                        for cc in range(2):
                            cs = slice(cc * 64, (cc + 1) * 64)
                            pq = self.ps()
                            c.mm(pq[0:64, 0:64], k4[dr][:, h, cs], q4[dr][:, h, cs])
                            pg = self.ps()
                            c.mm(pg[0:64, 0:64], ones[:, 0:64], rhsG[dr][h][:, cs], start=True, stop=False)
                            c.mm(pg[0:64, 0:64], ident[0:64, 0:64], NIs[dr].full(), start=False, stop=True)
                            c.act_fn(DT[dr][h][cc].full(), pg[0:64, 0:64], AF.Exp, bias=nsm[dr][0:64, 4 + cc * 4 + h:5 + cc * 4 + h])
                            c.tt(aT[dr][h][cc].full(), pq[0:64, 0:64], DT[dr][h][cc].full(), ALU.mult)
                            ksrc = kb_[dr][0:64, h, :] if cc == 0 else k1_[dr][:, h, :]
                            c.ts(kd[dr][h][cc].full(), ksrc, kds[dr][:, cc * 4 + h:cc * 4 + h + 1], ALU.mult)
                for step in range(2):
                    for dr in range(2):
                        cc = corders[dr][step]
                        cs = slice(cc * 64, (cc + 1) * 64)
                        pws, pos_ = [], []
                        for h in range(4):
                            pw = self.ps()
                            c.mm(pw[0:64, 0:128], wT[dr][h][:, cs], St[dr][h].full())
                            po = self.ps()
                            c.mm(po[0:64, 0:128], q4[dr][:, h, cs], St[dr][h].full())
                            pws.append(pw)
                            pos_.append(po)
                        for h in range(4):
                            c.tt(vn[dr][h].full(), ut[dr][h][cc].full(), pws[h][0:64, 0:128], ALU.subtract)
                            c.act_fn(o1[dr][h].full(), pos_[h][0:64, 0:128], AF.Copy, scale=esm[dr][0:64, 4 + cc * 4 + h:5 + cc * 4 + h]) \
                                if False else c.ts(o1[dr][h].full(), pos_[h][0:64, 0:128], esm[dr][0:64, 4 + cc * 4 + h:5 + cc * 4 + h], ALU.mult)
                        pas, pss = [], []
                        for h in range(4):
                            pa = self.ps()
                            c.mm(pa[0:64, 0:128], aT[dr][h][cc].full(), vn[dr][h].full())
                            psn = self.ps()
                            c.mm(psn[:, 0:128], kd[dr][h][cc].full(), vn[dr][h].full())
                            pas.append(pa)
                            pss.append(psn)
                        for h in range(4):
                            c.tt(ob[dr][par][cc][:, h, :], o1[dr][h].full(), pas[h][0:64, 0:128], ALU.add)
                            c.stt(St[dr][h].full(), St[dr][h].full(), esm[dr][:, 12 + cc * 4 + h:13 + cc * 4 + h], pss[h][:, 0:128],
                                  ALU.mult, ALU.add)
                for dr in range(2):
                    t0 = blk[dr] * 128
                    if not comb[dr]:
                        for cc in range(2):
                            c.store(otk[dr][t0 + cc * 64:t0 + (cc + 1) * 64], ob[dr][par][cc].full(), dram_w=otn[dr])
                        continue
                    for cc in range(2):
                        o_ = ob[dr][par][cc]
                        c.tt(o_.full(), o_.full(), ofl[dr][cc].full(), ALU.add)
                        c.tt(osq.full(), o_.full(), o_.full(), ALU.mult)
                        c.op(c.dve, lambda: self.nc.vector.tensor_reduce(oss.h[:], osq.h[:], mybir.AxisListType.X, ALU.add),
                             [osq.full()], [oss.full()])
                        c.act_fn(oss.full(), oss.full(), AF.Sqrt, bias=K["eps"][0:64, :], scale=1.0 / 128)
                        c.recip(oss.full(), oss.full())
                        for h in range(4):
                            c.ts(o_[:, h, :], o_[:, h, :], oss[:, h:h + 1], ALU.mult)
                            pt = self.ps()
                            c.transpose(pt[:, 0:64], o_[:, h, :], ident[0:64, 0:64])
                            c.stt(yT[dr][:, h, cc * 64:(cc + 1) * 64], pt[:, 0:64], onorm.full(),
                                  zb[dr][:, h, cc * 64:(cc + 1) * 64], ALU.mult, ALU.mult)
                    c.store(yTv[:, :, t0:t0 + 128], yT[dr].full(), dram_w="ydnT")
            c.barrier()

    def phase_p4(self, l):
        c, K, S = self.c, self.K, self.scr
        mv = self.modv[l]
        last = l == DEPTH - 1
        xsrc = (self.inp["xT0"] if l == 0 else S["xT1"]).rearrange("(kc p) t -> p kc t", p=128)
        xdst = S["xT1"].rearrange("(kc p) t -> p kc t", p=128)
        odst = self.outT.rearrange("(kc p) t -> p kc t", p=128)
        wo = self.inp["w_o_t"][l]
        wf1 = self.inp["w_ffn_in_t"][l]
        wf2 = self.inp["w_ffn_out_t"][l]
        with ExitStack() as ph:
            sb = lambda shp, nm: c.sbuf(shp, name=f"q{nm}", stack=ph)
            X = sb([128, KC, NB], "X")
            Hb = c.sbuf([128, KC, NB], dtype=MMT, name="qHb", stack=ph)
            hid = c.sbuf([128, FC, NB], dtype=MMT, name="qhid", stack=ph)
            NW4 = 6 if MMT == BF16 else 3
            W = [c.sbuf([128, KC, 512], dtype=MMT, name=f"qW{i}", stack=ph) for i in range(NW4)]
            W2 = [c.sbuf([128, FC, 128], dtype=MMT, name=f"qW2{i}", stack=ph) for i in range(2)]
            Yd = c.sbuf([128, 4, NB], dtype=MMT, name="qYd", stack=ph)
            Yf = c.sbuf([128, 4, NB], dtype=MMT, name="qYf", stack=ph)
            gm = [[sb([128, NB], f"gm{i}{j}") for j in range(3)] for i in range(2)]
            tmp = [sb([128, NB], f"tmp{i}") for i in range(3)]
            sq0, sq1, rstd = sb([128, NB], "sq0"), sb([128, NB], "sq1"), sb([128, NB], "rstd")
            fing = sb([128, 8], "fing")
            c.load(fing.full(), self.inp["fingT"])
            wi = [0]

            def next_w(src_ap):
                wt = W[wi[0] % NW4]
                wi[0] += 1
                c.load_r(wt.full(), src_ap, dram_r="wb")
                return wt

            PRE = (MMT == BF16)
            if PRE:
                jobs = [(self.inp["w_dn_t"][l], S["wb_dn"]), (self.inp["w_fn_t"][l], S["wb_fn"])]
                jobs += [(wo[h_], S["wb_o"][h_]) for h_ in range(2)]
                jobs += [(wf1[g_], S["wb_f1"][g_]) for g_ in range(FC // 2)]
                for (src_, dst_) in jobs:
                    wt = W[wi[0] % NW4]
                    wi[0] += 1
                    c.load_r(wt.full(), src_)
                    c.store(dst_, wt.full(), dram_w="wb")
                for m_ in range(8):
                    w2 = W2[m_ % 2]
                    c.load_r(w2.full(), wf2[m_])
                    c.store(S["wb_f2"][m_], w2.full(), dram_w="wb")
                wdn_s, wfn_s, wo, wf1, wf2 = S["wb_dn"], S["wb_fn"], S["wb_o"], S["wb_f1"], S["wb_f2"]
            else:
                wdn_s, wfn_s = self.inp["w_dn_t"][l], self.inp["w_fn_t"][l]
            nblk = 0
            for (t0, w, _) in blocks_of(not last):
                col = 1 if t0 == 0 else 0
                c.load(X[:, :, :w], xsrc[:, :, t0:t0 + w], dram_r="xT1")
                c.load_r(Yd[:, :, :w], S["ydnT"].rearrange("h p t -> p h t")[:, :, t0:t0 + w], dram_r="ydnT")
                c.load_r(Yf[:, :, :w], S["YT"].rearrange("h p t -> p h t")[:, :, t0:t0 + w], dram_r="YT")
                Wdn = next_w(wdn_s)
                Wfn = next_w(wfn_s)
                for mc in range(8):
                    g_ = gm[mc % 2]
                    c.load(g_[0][:, :w], S["g0T"][mc, :, t0:t0 + w], dram_r="g0T")
                    c.load(g_[1][:, :w], S["g2T"][mc, :, t0:t0 + w], dram_r="g2T")
                    c.load(g_[2][:, :w], S["m1T"][mc, :, t0:t0 + w], dram_r="m1T")
                    pd = self.ps()
                    pf = self.ps()
                    for kc in range(4):
                        c.mm(pd[:, :w], Wdn[:, (mc // 4) * 4 + kc, (mc % 4) * 128:(mc % 4 + 1) * 128], Yd[:, kc, :w], start=(kc == 0), stop=(kc == 3), r=True)
                    for kc in range(4):
                        c.mm(pf[:, :w], Wfn[:, (mc // 4) * 4 + kc, (mc % 4) * 128:(mc % 4 + 1) * 128], Yf[:, kc, :w], start=(kc == 0), stop=(kc == 3), r=True)
                    c.tt(g_[0][:, :w], g_[0][:, :w], pd[:, :w], ALU.mult)
                    c.tt(g_[1][:, :w], g_[1][:, :w], pf[:, :w], ALU.mult)
                    c.tt(g_[0][:, :w], g_[0][:, :w], g_[1][:, :w], ALU.add)
                    c.tt(Hb[:, mc, :w].r(), g_[0][:, :w], g_[2][:, :w], ALU.add)
                for half in range(2):
                    wt = next_w(wo[half])
                    for j in range(4):
                        mc = half * 4 + j
                        p = self.ps()
                        for kc in range(KC):
                            c.mm(p[:, :w], wt[:, kc, j * 128:(j + 1) * 128], Hb[:, kc, :w], start=(kc == 0), stop=(kc == KC - 1), r=True)
                        c.stt(X[:, mc, :w], p[:, :w], mv["G1"][:, mc, col:col + 1], X[:, mc, :w], ALU.mult, ALU.add)
                if nblk == 1:
                    self.dbg(f"xmid{l}", X.full(), [128, KC, NB])
                self.norm_mod(X, Hb, mv["A2"], mv["B2"], col, w, (sq0, sq1, rstd))
                for gi in range(FC // 2):
                    wt = next_w(wf1[gi])
                    for j in range(2):
                        fcx = gi * 2 + j
                        pa = self.ps()
                        pb = self.ps()
                        for kc in range(KC):
                            c.mm(pa[:, :w], wt[:, kc, j * 128:(j + 1) * 128], Hb[:, kc, :w], start=(kc == 0), stop=(kc == KC - 1), r=True)
                        for kc in range(KC):
                            c.mm(pb[:, :w], wt[:, kc, 256 + j * 128:256 + (j + 1) * 128], Hb[:, kc, :w], start=(kc == 0), stop=(kc == KC - 1), r=True)
                        tm = tmp[fcx % 3]
                        c.act_fn(tm[:, :w], pa[:, :w], AF.Silu)
                        c.tt(hid[:, fcx, :w].r(), tm[:, :w], pb[:, :w], ALU.mult)
                for mc in range(8):
                    w2 = W2[mc % 2]
                    c.load_r(w2.full(), wf2[mc], dram_r="wb")
                    p = self.ps()
                    for fcx in range(FC):
                        c.mm(p[:, :w], w2[:, fcx, :], hid[:, fcx, :w], start=(fcx == 0), stop=(fcx == FC - 1), r=True)
                    c.stt(X[:, mc, :w], p[:, :w], mv["G2"][:, mc, col:col + 1], X[:, mc, :w], ALU.mult, ALU.add)
                if not last:
                    c.store(xdst[:, :, t0:t0 + w], X[:, :, :w], dram_w="xT1")
                else:
                    p = self.ps()
                    for kc in range(KC):
                        sq = sq0 if kc % 2 == 0 else sq1
                        c.tt(sq[:, :w], X[:, kc, :w], X[:, kc, :w], ALU.mult)
                        c.mm(p[:, :w], K["ones"].full(), sq[:, :w], start=(kc == 0), stop=(kc == KC - 1))
                    c.act_fn(rstd[:, :w], p[:, :w], AF.Sqrt, bias=K["eps"].full(), scale=1.0 / D)
                    c.recip(rstd[:, :w], rstd[:, :w])
                    for kc in range(KC):
                        c.stt(X[:, kc, :w], X[:, kc, :w], fing[:, kc:kc + 1], rstd[:, :w], ALU.mult, ALU.mult)
                    c.store(odst[:, :, t0 - CTX:t0 - CTX + w], X[:, :, :w])
                nblk += 1
                if self.max_blocks and nblk >= self.max_blocks:
                    break
            c.barrier()

_CACHE = {}


def kernel(**inputs):
    inp = {k: np.asarray(v) for k, v in inputs.items()}
    consts = make_consts()
    w = host_layout(inp)
    if "nc" not in _CACHE:
        dry = Builder()
        dry.build()
        _CACHE["nc"] = Builder(needed=dry.c.needed).build()
    nc = _CACHE["nc"]
    in_maps = []
    for b in range(NCORES):
        xT0 = np.ascontiguousarray(np.concatenate([inp["ctx"][b], inp["x"][b]], 0).T, dtype=np.float32)
        cc = np.stack([inp["c"][b].reshape(8, 128).T, inp["c_ctx"].reshape(8, 128).T], -1).astype(np.float32)
        m = dict(xT0=xT0, ccT=np.ascontiguousarray(cc))
        m.update(consts)
        m.update(w)
        in_maps.append(m)
    res = run_bass_kernel_spmd(nc, in_maps, core_ids=list(range(NCORES)))
    out = np.stack([np.ascontiguousarray(r["outT"].T) for r in res.results], 0)
    return out.astype(np.float32)
```

```python
import math
from contextlib import ExitStack
import numpy as np
import concourse.bass as bass
import concourse.mybir as mybir
from concourse.bass_utils import run_bass_kernel_spmd

F32 = mybir.dt.float32
F32R = mybir.dt.float32r
BF16 = mybir.dt.bfloat16
FAST_MM = True
MM_BF16 = True
MMT = BF16 if (FAST_MM and MM_BF16) else F32
ALU = mybir.AluOpType
AF = mybir.ActivationFunctionType

D = 1024
KC = 8
SEQ = 8192
CTX = 256
T = SEQ + CTX
DEPTH = 2
NB = 512
NIN = 4112
DFF = 2816
FC = 22
EPS = 1e-6
BIG = 30000.0
NCORES = 8


class Tile:
    def __init__(self, ctx, handle, name):
        self.ctx = ctx
        self.h = handle
        self.name = name
        self.w = None
        self.r = {}

    def __getitem__(self, idx):
        return TV(self, self.h[idx])

    def full(self):
        return TV(self, self.h[:])


class TV:
    def __init__(self, tile, ap):
        self.t = tile
        self.ap = ap

    def re(self, pat, **kw):
        return TV(self.t, self.ap.rearrange(pat, **kw))

    def r(self):
        if not FAST_MM or self.ap.dtype != F32:
            return self
        return TV(self.t, self.ap.bitcast(F32R))

    def __getitem__(self, idx):
        return TV(self.t, self.ap[idx])


def _ap(x):
    return x.ap if isinstance(x, TV) else x


class Engine:
    def __init__(self, ctx, name, eng):
        self.name = name
        self.eng = eng
        self.sem = ctx.new_sem("s_" + name)
        self.n = 0
        self.inc = 0
        self.val = {}
        self.seen = {}


class Ctx:
    def __init__(self, nc, stack, needed=None):
        self.nc = nc
        self.stack = stack
        self.needed_in = needed
        self.needed = set()
        self.sems = {}
        self.pe = Engine(self, "pe", nc.tensor)
        self.act = Engine(self, "act", nc.scalar)
        self.dve = Engine(self, "dve", nc.vector)
        self.pool = Engine(self, "pool", nc.gpsimd)
        self.sp = Engine(self, "sp", nc.sync)
        self.engs = {e.sem: e for e in (self.pe, self.act, self.dve, self.pool, self.sp)}
        self.dma_rings = {}
        for e in (self.sp, self.pool):
            ring = [self.new_sem(f"d_{e.name}{i}") for i in range(16)]
            self.dma_rings[e.name] = dict(sems=ring, vals=[0] * len(ring), i=0)
        self.dram_w = {}
        self.ntile = 0
        self.ninst = 0

    def new_sem(self, name):
        s = self.stack.enter_context(self.nc.semaphore(name))
        self.sems[name] = s
        return name

    def sbuf(self, shape, dtype=F32, name=None, stack=None):
        self.ntile += 1
        name = f"{name or 't'}_{self.ntile}"
        h = (stack or self.stack).enter_context(self.nc.sbuf_tensor(name, list(shape), dtype))
        return Tile(self, h, name)

    def psum(self, shape, dtype=F32, name=None, stack=None):
        self.ntile += 1
        name = name or f"p{self.ntile}"
        h = (stack or self.stack).enter_context(self.nc.psum_tensor(name, list(shape), dtype))
        return Tile(self, h, name)

    def _wait(self, E, ticket):
        if ticket is None:
            return
        key, val = ticket
        if E.seen.get(key, 0) >= val:
            return
        self.needed.add(ticket)
        real = self.engs[key].val[val] if key in self.engs else val
        E.eng.wait_ge(self.sems[key], real)
        E.seen[key] = val

    def _deps(self, E, reads, writes, same_ok=False):
        for tv in reads:
            if isinstance(tv, TV):
                t = tv.t
                if t.w is not None and not (same_ok and t.w[0] == E.sem):
                    self._wait(E, t.w)
        for tv in writes:
            if isinstance(tv, TV):
                t = tv.t
                if t.w is not None and not (same_ok and t.w[0] == E.sem):
                    self._wait(E, t.w)
                for key, val in t.r.items():
                    if same_ok and key == E.sem:
                        continue
                    self._wait(E, (key, val))

    def _commit(self, ticket, reads, writes):
        for tv in writes:
            if isinstance(tv, TV):
                tv.t.w = ticket
                tv.t.r = {}
        for tv in reads:
            if isinstance(tv, TV):
                t = tv.t
                if t.r.get(ticket[0], 0) < ticket[1]:
                    t.r[ticket[0]] = ticket[1]

    def op(self, E, fn, reads, writes, same_ok=False):
        self._deps(E, reads, writes, same_ok)
        inst = fn()
        E.n += 1
        ticket = (E.sem, E.n)
        if self.needed_in is None or ticket in self.needed_in:
            inst.then_inc(self.sems[E.sem], 1)
            E.inc += 1
            E.val[E.n] = E.inc
        self.ninst += 1
        self._commit(ticket, reads, writes)
        return ticket

    def dma(self, E, out, in_, dram_r=None, dram_w=None, nocast=False, **kw):
        ring = self.dma_rings[E.name]
        i = ring["i"]
        ring["i"] = (i + 1) % len(ring["sems"])
        key = ring["sems"][i]
        if ring["vals"][i] > 0:
            self._wait(E, (key, ring["vals"][i]))
        reads = [in_] if isinstance(in_, TV) else []
        writes = [out] if isinstance(out, TV) else []
        self._deps(E, reads, writes)
        if dram_r is not None:
            for nm in (dram_r if isinstance(dram_r, (list, tuple)) else [dram_r]):
                for tk in self.dram_w.get(nm, []):
                    self._wait(E, tk)
        oa, ia = _ap(out), _ap(in_)
        if oa.dtype == F32R and ia.dtype != F32R and not nocast:
            ia = ia.bitcast(F32R)
        inst = E.eng.dma_start(out=oa, in_=ia, **kw)
        ring["vals"][i] += 16
        inst.then_inc(self.sems[key], 16)
        ticket = (key, ring["vals"][i])
        self.ninst += 1
        self._commit(ticket, reads, writes)
        if dram_w is not None:
            lst = self.dram_w.setdefault(dram_w, [])
            lst[:] = [tk for tk in lst if tk[0] != key] + [ticket]
        return ticket

    def load(self, out, in_, dram_r=None, **kw):
        return self.dma(self.sp, out, in_, dram_r=dram_r, **kw)

    def store(self, out, in_, dram_w=None, **kw):
        return self.dma(self.sp, out, in_, dram_w=dram_w, **kw)

    def load_r(self, out, in_, dram_r=None, **kw):
        if not FAST_MM:
            return self.dma(self.pool, out, in_, dram_r=dram_r, **kw)
        return self.dma(self.pool, out.r(), in_, dram_r=dram_r, nocast=True, **kw)

    def mm(self, out, lhsT, rhs, start=True, stop=True, r=False):
        if r and FAST_MM and _ap(lhsT).dtype == F32:
            la, ra = _ap(lhsT).bitcast(F32R), _ap(rhs).bitcast(F32R)
        else:
            la, ra = _ap(lhsT), _ap(rhs)
        return self.op(self.pe, lambda: self.nc.tensor.matmul(_ap(out), la, ra, start=start, stop=stop),
                       [lhsT, rhs], [out], same_ok=True)

    def transpose(self, out, in_, ident):
        return self.op(self.pe, lambda: self.nc.tensor.transpose(_ap(out), _ap(in_), _ap(ident)),
                       [in_, ident], [out], same_ok=True)

    def act_fn(self, out, in_, func, bias=None, scale=None):
        kw = {}
        reads = [in_]
        if bias is not None:
            kw["bias"] = _ap(bias)
            reads.append(bias)
        if scale is not None:
            kw["scale"] = _ap(scale)
            reads.append(scale)
        return self.op(self.act, lambda: self.nc.scalar.activation(_ap(out), _ap(in_), func, **kw), reads, [out])

    def _ve(self, E):
        return self.nc.vector if E is self.dve else self.nc.gpsimd

    def tt(self, out, in0, in1, op, E=None):
        E = E or self.dve
        return self.op(E, lambda: self._ve(E).tensor_tensor(_ap(out), _ap(in0), _ap(in1), op), [in0, in1], [out])

    def ts(self, out, in0, s1, op0, s2=None, op1=None, E=None):
        E = E or self.dve
        reads = [in0] + [s for s in (s1, s2) if isinstance(s, TV)]
        if op1 is None:
            return self.op(E, lambda: self._ve(E).tensor_scalar(_ap(out), _ap(in0), _ap(s1), None, op0), reads, [out])
        return self.op(E, lambda: self._ve(E).tensor_scalar(_ap(out), _ap(in0), _ap(s1), _ap(s2), op0, op1), reads, [out])

    def stt(self, out, in0, scalar, in1, op0, op1, E=None):
        E = E or self.dve
        reads = [in0, in1] + ([scalar] if isinstance(scalar, TV) else [])
        return self.op(E, lambda: self._ve(E).scalar_tensor_tensor(_ap(out), _ap(in0), _ap(scalar), _ap(in1), op0, op1),
                       reads, [out])

    def copy(self, out, in_, E=None):
        E = E or self.dve
        if E is self.act:
            return self.op(E, lambda: self.nc.scalar.copy(_ap(out), _ap(in_)), [in_], [out])
        return self.op(E, lambda: self._ve(E).tensor_copy(_ap(out), _ap(in_)), [in_], [out])

    def recip(self, out, in_):
        return self.op(self.dve, lambda: self.nc.vector.reciprocal(_ap(out), _ap(in_)), [in_], [out])

    def memset(self, out, val, E=None):
        E = E or self.dve
        return self.op(E, lambda: self._ve(E).memset(_ap(out), val), [], [out])

    def barrier(self):
        tickets = []
        for name, ring in self.dma_rings.items():
            for key, val in zip(ring["sems"], ring["vals"]):
                if val > 0:
                    tickets.append((key, val))
        for e in (self.pe, self.act, self.dve, self.pool):
            if e.n > 0:
                tickets.append((e.sem, e.n))
        for E in (self.pe, self.act, self.dve, self.pool, self.sp):
            for tk in tickets:
                if tk[0] != E.sem:
                    self._wait(E, tk)

    def finish(self):
        E = self.sp
        for name, ring in self.dma_rings.items():
            for key, val in zip(ring["sems"], ring["vals"]):
                if val > 0:
                    self._wait(E, (key, val))
        for e in (self.pe, self.act, self.dve, self.pool):
            if e.n > 0:
                self._wait(E, (e.sem, e.n))


def make_consts():
    c = {}
    c["ident"] = np.eye(128, dtype=np.float32)
    c["ones"] = np.ones((128, 128), np.float32)
    idx = np.arange(128)
    same = (idx[:, None] // 64) == (idx[None, :] // 64)
    c["Uf"] = (same & (idx[:, None] <= idx[None, :])).astype(np.float32)
    c["Ub"] = (same & (idx[:, None] >= idx[None, :])).astype(np.float32)
    c["NSf"] = np.where(same & (idx[:, None] > idx[None, :]), 0.0, -BIG).astype(np.float32)
    c["NSb"] = np.where(same & (idx[:, None] < idx[None, :]), 0.0, -BIG).astype(np.float32)
    i64 = np.arange(64)
    c["NIf"] = np.where(i64[None, :] >= i64[:, None], 0.0, -BIG).astype(np.float32)
    c["NIb"] = np.where(i64[None, :] <= i64[:, None], 0.0, -BIG).astype(np.float32)
    sel = np.zeros((128, 2, 128), np.float32)
    sel[:64, 0, :] = 1.0
    sel[64:, 1, :] = 1.0
    c["SelC"] = sel
    n = np.arange(128)
    ang = 2 * np.pi * np.outer(n, n) / 128.0
    Cc, Sc = np.cos(ang), np.sin(ang)
    sl = 1.0 / math.sqrt(SEQ * 128.0)
    c["R0"] = (np.concatenate([Cc, -Sc, -Sc, -Cc], 1) * sl).astype(np.float32)
    sc_ = 1.0 / math.sqrt(CTX * 128.0)
    c["R0c"] = (np.concatenate([Cc, -Sc], 1) * sc_).astype(np.float32)
    c["C128"] = Cc.astype(np.float32)
    c["S128"] = Sc.astype(np.float32)
    N2 = SEQ // 128
    k1 = np.arange(128)[:, None]
    n2 = np.arange(N2)[None, :]
    th = 2 * np.pi * k1 * n2 / float(SEQ)
    c["twr"] = np.cos(th).astype(np.float32)
    c["twi"] = (-np.sin(th)).astype(np.float32)
    c["ntwi"] = np.sin(th).astype(np.float32)
    k2 = np.arange(N2)[None, :]
    n2c = np.arange(N2)[:, None]
    a2 = 2 * np.pi * n2c * k2 / float(N2)
    c["G64"] = np.concatenate([np.cos(a2), np.sin(a2)], 0).astype(np.float32)
    m = np.arange(256)
    a3 = 2 * np.pi * np.outer(m, m) / 256.0
    c["C256"] = np.cos(a3).astype(np.float32)
    c["S256"] = np.sin(a3).astype(np.float32)
    return c


CONST_SHAPES = {k: v.shape for k, v in make_consts().items()}


def configure(seq):
    global SEQ, T, CONST_SHAPES
    SEQ = seq
    T = SEQ + CTX
    CONST_SHAPES = {k: v.shape for k, v in make_consts().items()}

WIN_OFFS = (0, 512, 1024, 1536, 2064, 2576, 3088, 3600)

WEIGHT_SHAPES = {
    "w_mod": (DEPTH, D, 6 * D), "bmodT": (DEPTH, 128, 48), "n1gT": (DEPTH, 128, 8), "n2gT": (DEPTH, 128, 8),
    "w_in_t": (DEPTH, 8, 128, 8, 512), "w_ba": (DEPTH, 128, 8, 16),
    "dncwT": (DEPTH, 128, 12, 3), "dn_neg_alog": (DEPTH, 128, 8), "dn_dtb": (DEPTH, 128, 8),
    "onormT": (DEPTH, 128, 1), "w_dn_t": (DEPTH, 128, 8, 512), "sccwT": (DEPTH, 128, 4, 3), "w_sc_t": (DEPTH, 128, 4, D),
    "w_fn_t": (DEPTH, 128, 8, 512), "w_gate_t": (DEPTH, 6, 128, 8, 512), "bgateT": (DEPTH, 128, 24), "w_o_t": (DEPTH, 2, 128, 8, 512),
    "w_ffn_in_t": (DEPTH, FC // 2, 128, 8, 512), "w_ffn_out_t": (DEPTH, 8, 128, FC, 128), "fingT": (128, 8),
}
R_WEIGHTS = ("w_in_t", "w_dn_t", "w_sc_t", "w_fn_t", "w_gate_t", "w_o_t", "w_ffn_in_t", "w_ffn_out_t")


def host_layout(inp):
    f = lambda a: np.ascontiguousarray(a, dtype=np.float32)
    fm = lambda v, nch: f(np.asarray(v).reshape(nch, 128).T)

    def ktile(wm):
        wm = np.asarray(wm)
        return wm.reshape(wm.shape[0] // 128, 128, wm.shape[1]).transpose(1, 0, 2)

    def halves(wm):
        t = ktile(wm)
        return np.concatenate([t[:, :, 0:512], t[:, :, 512:1024]], 1)

    w = {}
    L = range(DEPTH)
    w["w_mod"] = f(inp["w_mod"])
    w["bmodT"] = f(np.stack([fm(inp["b_mod"][l], 48) for l in L]))
    w["n1gT"] = f(np.stack([fm(inp["norm1_g"][l], 8) for l in L]))
    w["n2gT"] = f(np.stack([fm(inp["norm2_g"][l], 8) for l in L]))
    w["w_in_t"] = f(np.stack([np.stack([ktile(inp["w_in"][l][:, o:o + 512]) for o in WIN_OFFS]) for l in L]))
    w["w_ba"] = f(np.stack([ktile(inp["w_in"][l][:, 2048:2064]) for l in L]))
    w["dncwT"] = f(np.stack([np.asarray(inp["dn_conv_w"][l]).T.reshape(12, 128, 3).transpose(1, 0, 2) for l in L]))
    w["dn_neg_alog"] = f(np.stack([np.broadcast_to(np.asarray(inp["dn_a_log"][l]).reshape(1, 8), (128, 8)) for l in L]))
    w["dn_dtb"] = f(np.stack([np.broadcast_to(np.asarray(inp["dn_dt_bias"][l]).reshape(1, 8), (128, 8)) for l in L]))
    w["onormT"] = f(np.stack([np.asarray(inp["dn_onorm_g"][l]).reshape(128, 1) for l in L]))
    w["w_dn_t"] = f(np.stack([halves(inp["w_dn_out"][l]) for l in L]))
    w["sccwT"] = f(np.stack([np.asarray(inp["sc_conv_w"][l]).T.reshape(4, 128, 3).transpose(1, 0, 2) for l in L]))
    w["w_sc_t"] = f(np.stack([ktile(inp["w_sc_out"][l]) for l in L]))
    w["w_fn_t"] = f(np.stack([halves(inp["w_fn_out"][l]) for l in L]))
    w["w_gate_t"] = f(np.stack([np.stack([ktile(inp["w_gate"][l][:, g * 512:(g + 1) * 512]) for g in range(6)]) for l in L]))
    w["bgateT"] = f(np.stack([fm(inp["b_gate"][l], 24) for l in L]))
    w["w_o_t"] = f(np.stack([np.stack([ktile(inp["w_o"][l][:, g * 512:(g + 1) * 512]) for g in range(2)]) for l in L]))
    w["w_ffn_in_t"] = f(np.stack([np.stack([np.concatenate([ktile(inp["w_ffn_in"][l][:, g * 256:(g + 1) * 256]),
                                                            ktile(inp["w_ffn_in"][l][:, DFF + g * 256:DFF + (g + 1) * 256])], 2)
                                            for g in range(FC // 2)]) for l in L]))
    w["w_ffn_out_t"] = f(np.stack([np.stack([ktile(inp["w_ffn_out"][l][:, m * 128:(m + 1) * 128]) for m in range(8)]) for l in L]))
    w["fingT"] = fm(inp["final_g"], 8)
    return w


def blocks_of(include_ctx=True):
    bl = []
    if include_ctx:
        bl.append((0, CTX, CTX))
    for i in range(SEQ // NB):
        bl.append((CTX + i * NB, NB, 64))
    return bl


class Builder:
    def __init__(self, debug=None, stop_after=None, layers=DEPTH, max_blocks=None, needed=None):
        self.max_blocks = max_blocks
        self.needed = needed
        self.debug = debug or []
        self.stop_after = stop_after
        self.layers = layers
        self.nc = bass.Bass("TRN2", target_bir_lowering=False)
        nc = self.nc
        self.inp = {}
        self.inp["xT0"] = nc.dram_tensor("xT0", [D, T], F32, kind="ExternalInput").ap()
        self.inp["ccT"] = nc.dram_tensor("ccT", [128, 8, 2], F32, kind="ExternalInput").ap()
        for k, shp in CONST_SHAPES.items():
            self.inp[k] = nc.dram_tensor(k, list(shp), F32, kind="ExternalInput").ap()
        for k, shp in WEIGHT_SHAPES.items():
            self.inp[k] = nc.dram_tensor(k, list(shp), F32, kind="ExternalInput").ap()
        self.outT = nc.dram_tensor("outT", [D, SEQ], F32, kind="ExternalOutput").ap()
        self.scr = {}
        for name, shp in dict(
            qT=[4, 128, T], kT=[4, 128, T], ktok=[T, 4, 128], vtok=[T, 4, 128], zT=[4, 128, T], fnT=[4, 128, T],
            m1T=[8, 128, T], g0T=[8, 128, T], g2T=[8, 128, T], bg=[T, 16], otok=[T, 4, 128], otokb=[T, 4, 128], ydnT=[4, 128, T],
            YT=[4, 128, T], Ap=[4, 2 * (SEQ // 128), 128, 128], xT1=[D, T],
        ).items():
            kind = "ExternalOutput" if name in self.debug else "Internal"
            self.scr[name] = nc.dram_tensor("s_" + name, shp, F32, kind=kind).ap()
        if MMT == BF16:
            for name, shp in dict(wb_dn=[128, 8, 512], wb_fn=[128, 8, 512], wb_o=[2, 128, 8, 512],
                                  wb_f1=[FC // 2, 128, 8, 512], wb_f2=[8, 128, FC, 128]).items():
                self.scr[name] = nc.dram_tensor("s_" + name, shp, BF16, kind="Internal").ap()

    def build(self):
        with ExitStack() as st:
            self.c = Ctx(self.nc, st, needed=self.needed)
            self._consts(st)
            try:
                for l in range(self.layers):
                    self.layer(l)
            except StopIteration:
                pass
            self.c.finish()
        return self.nc

    def dbg(self, name, tv, shape):
        if name not in self.debug:
            return
        d = self.nc.dram_tensor("dbg_" + name, list(shape), F32, kind="ExternalOutput").ap()
        self.c.store(d, tv)

    def _stop(self, tag):
        if self.stop_after == tag:
            raise StopIteration

    def _consts(self, st):
        c = self.c
        self.K = {}
        for k in ("ident", "ones", "Uf", "Ub", "NSf", "NSb"):
            t = c.sbuf([128, 128], name="k_" + k)
            c.load(t.full(), self.inp[k])
            self.K[k] = t
        for k in ("NIf", "NIb"):
            t = c.sbuf([64, 64], name="k_" + k)
            c.load(t.full(), self.inp[k])
            self.K[k] = t
        t = c.sbuf([128, 2, 128], name="k_SelC")
        c.load(t.full(), self.inp["SelC"])
        self.K["SelC"] = t
        self.K["eps"] = c.sbuf([128, 1], name="k_eps")
        c.memset(self.K["eps"].full(), EPS)
        self.K["negones"] = c.sbuf([128, 128], name="k_negones")
        c.memset(self.K["negones"].full(), -1.0)
        self.cc = c.sbuf([128, 8, 2], name="cc")
        self.modv = [dict() for _ in range(DEPTH)]
        for l in range(DEPTH):
            for nm in ("A1", "B1", "G1", "A2", "B2", "G2"):
                self.modv[l][nm] = c.sbuf([128, 8, 2], name=f"{nm}_{l}")
        self.PS = [c.psum([128, 512], name=f"ps{i}") for i in range(8)]
        self.psi = 0

    def ps(self):
        p = self.PS[self.psi % 8]
        self.psi += 1
        return p

    def layer(self, l):
        self.phase_mod(l)
        self._stop(f"mod{l}")
        self.phase_p1(l)
        self._stop(f"p1_{l}")
        self.phase_fnet(l)
        self._stop(f"p2_{l}")
        self.phase_dn(l)
        self._stop(f"p3_{l}")
        self.phase_p4(l)
        self._stop(f"p4_{l}")

    def phase_mod(self, l):
        c, K = self.c, self.K
        if l == 0:
            c.load(self.cc.full(), self.inp["ccT"])
            c.act_fn(self.cc.full(), self.cc.full(), AF.Silu)
        with ExitStack() as ph:
            modT = c.sbuf([128, 48, 2], name=f"modT{l}", stack=ph)
            bmod = c.sbuf([128, 48], name=f"bmod{l}", stack=ph)
            c.load(bmod.full(), self.inp["bmodT"][l])
            wts = [c.sbuf([128, 8, 512], name=f"wmod{i}", stack=ph) for i in range(2)]
            wsrc = self.inp["w_mod"][l].rearrange("(kc p) f -> p kc f", p=128)
            for gi in range(12):
                wt = wts[gi % 2]
                c.load(wt.full(), wsrc[:, :, gi * 512:(gi + 1) * 512])
                p = self.ps()
                for j in range(4):
                    for kc in range(8):
                        c.mm(p[:, j * 2:(j + 1) * 2], wt[:, kc, j * 128:(j + 1) * 128], self.cc[:, kc, :],
                             start=(kc == 0), stop=(kc == 7))
                for j in range(4):
                    fcx = gi * 4 + j
                    c.ts(modT[:, fcx, :], p[:, j * 2:(j + 1) * 2], bmod[:, fcx:fcx + 1], ALU.add)
            mv = self.modv[l]
            n1g = c.sbuf([128, 8], name=f"n1g{l}", stack=ph)
            n2g = c.sbuf([128, 8], name=f"n2g{l}", stack=ph)
            c.load(n1g.full(), self.inp["n1gT"][l])
            c.load(n2g.full(), self.inp["n2gT"][l])
            for col in range(2):
                c.copy(mv["B1"][:, :, col], modT[:, 0:8, col])
                c.stt(mv["A1"][:, :, col], modT[:, 8:16, col], 1.0, n1g.full(), ALU.add, ALU.mult)
                c.copy(mv["G1"][:, :, col], modT[:, 16:24, col])
                c.copy(mv["B2"][:, :, col], modT[:, 24:32, col])
                c.stt(mv["A2"][:, :, col], modT[:, 32:40, col], 1.0, n2g.full(), ALU.add, ALU.mult)
                c.copy(mv["G2"][:, :, col], modT[:, 40:48, col])
            self.dbg(f"modT{l}", modT.full(), [128, 48, 2])
            self.dbg(f"A1_{l}", mv["A1"].full(), [128, 8, 2])
            c.barrier()

    def run_units(self, units, nslots):
        active = []
        free = list(range(nslots))
        it = iter(units)
        pending = None
        done = False
        while True:
            while free and not done:
                u = pending if pending is not None else next(it, None)
                pending = None
                if u is None:
                    done = True
                    break
                if isinstance(u, str):
                    if active:
                        pending = u
                        break
                    continue
                sl = free.pop(0)
                active.append((sl, u(sl)))
            if not active:
                if done:
                    break
                continue
            for ent in list(active):
                try:
                    next(ent[1])
                except StopIteration:
                    active.remove(ent)
                    free.append(ent[0])
                    free.sort()

    def norm_mod(self, X, H, A, Bv, col, w, ph_tiles):
        c, K = self.c, self.K
        sq0, sq1, rstd = ph_tiles
        p = self.ps()
        for kc in range(KC):
            sq = sq0 if kc % 2 == 0 else sq1
            if kc % 2 == 0:
                c.act_fn(sq[:, :w], X[:, kc, :w], AF.Square)
            else:
                c.tt(sq[:, :w], X[:, kc, :w], X[:, kc, :w], ALU.mult)
            c.mm(p[:, :w], K["ones"].full(), sq[:, :w], start=(kc == 0), stop=(kc == KC - 1))
        c.act_fn(rstd[:, :w], p[:, :w], AF.Sqrt, bias=K["eps"].full(), scale=1.0 / D)
        c.recip(rstd[:, :w], rstd[:, :w])
        for kc in range(KC):
            tmp_ = sq0 if kc % 2 == 0 else sq1
            c.tt(tmp_[:, :w], X[:, kc, :w], rstd[:, :w], ALU.mult)
            c.ts(H[:, kc, :w].r(), tmp_[:, :w], A[:, kc, col:col + 1], ALU.mult, Bv[:, kc, col:col + 1], ALU.add)

    def conv3(self, out, x, wk, w, rowlen):
        c = self.c
        xv = x[:, :w].re("p (r t) -> p r t", t=rowlen)
        ov = out[:, :w].re("p (r t) -> p r t", t=rowlen)
        c.ts(out[:, :w], x[:, :w], wk[:, 1:2], ALU.mult)
        c.stt(ov[:, :, 1:rowlen], xv[:, :, 0:rowlen - 1], wk[:, 0:1], ov[:, :, 1:rowlen], ALU.mult, ALU.add)
        c.stt(ov[:, :, 0:rowlen - 1], xv[:, :, 1:rowlen], wk[:, 2:3], ov[:, :, 0:rowlen - 1], ALU.mult, ALU.add)

    def phase_p1(self, l):
        c, K, S = self.c, self.K, self.scr
        mv = self.modv[l]
        xsrc = (self.inp["xT0"] if l == 0 else S["xT1"]).rearrange("(kc p) t -> p kc t", p=128)
        win = self.inp["w_in_t"][l]
        wgate = self.inp["w_gate_t"][l]
        with ExitStack() as ph:
            X = c.sbuf([128, KC, NB], name="p1X", stack=ph)
            H = c.sbuf([128, KC, NB], dtype=MMT, name="p1H", stack=ph)
            RES = (MMT == BF16)
            NW1 = 14 if RES else 4
            W = [c.sbuf([128, KC, 512], dtype=MMT, name=f"p1W{i}", stack=ph) for i in range(NW1)]
            wi = [0]
            Wres = {}
            if RES:
                for gi in range(8):
                    Wres[("in", gi)] = W[gi]
                    c.load_r(W[gi].full(), win[gi])
                for g6 in range(6):
                    Wres[("g", g6 // 2, g6 % 2)] = W[8 + g6]
                    c.load_r(W[8 + g6].full(), wgate[g6])
            Wba = c.sbuf([128, KC, 16], dtype=MMT, name="p1Wba", stack=ph)
            (c.load_r if MMT != F32 else c.load)(Wba.full(), self.inp["w_ba"][l])
            Wsc = c.sbuf([128, 4, D], dtype=MMT, name="p1Wsc", stack=ph)
            c.load_r(Wsc.full(), self.inp["w_sc_t"][l])
            dncw = c.sbuf([128, 12, 3], name="p1dncw", stack=ph)
            c.load(dncw.full(), self.inp["dncwT"][l])
            sccw = c.sbuf([128, 4, 3], name="p1sccw", stack=ph)
            c.load(sccw.full(), self.inp["sccwT"][l])
            bgate = c.sbuf([128, 24], name="p1bgate", stack=ph)
            c.load(bgate.full(), self.inp["bgateT"][l])
            nalog = c.sbuf([128, 8], name="p1nalog", stack=ph)
            c.load(nalog.full(), self.inp["dn_neg_alog"][l])
            c.act_fn(nalog.full(), nalog.full(), AF.Exp)
            c.ts(nalog.full(), nalog.full(), -1.0, ALU.mult)
            dtb = c.sbuf([128, 8], name="p1dtb", stack=ph)
            c.load(dtb.full(), self.inp["dn_dtb"][l])
            xa = [c.sbuf([128, NB], name=f"p1xa{i}", stack=ph) for i in range(4)]
            xb = [c.sbuf([128, NB], name=f"p1xb{i}", stack=ph) for i in range(4)]
            stq = [c.sbuf([128, NB], name=f"p1st{i}", stack=ph) for i in range(4)]
            PSs = self.PS
            kfm = [c.sbuf([128, NB], name=f"p1kfm{j}", stack=ph) for j in range(4)]
            scin = kfm
            scf = [c.sbuf([128, NB], dtype=MMT, name=f"p1scf{j}", stack=ph) for j in range(4)]
            tok = [c.sbuf([128, 4, 128], name=f"p1tok{i}", stack=ph) for i in range(4)]
            vfm = [c.sbuf([128, NB], name=f"p1vfm{j}", stack=ph) for j in range(4)]
            bgt = c.sbuf([128, 4, 16], name="p1bg", stack=ph)
            bgx = c.sbuf([128, 4, 8], name="p1bgx", stack=ph)

            def next_w(src_ap):
                wt = W[wi[0] % NW1]
                wi[0] += 1
                c.load_r(wt.full(), src_ap)
                return wt

            def proj(wt, j, w, p=None):
                p = p or self.ps()
                for kc in range(KC):
                    c.mm(p[:, :w], wt[:, kc, j * 128:(j + 1) * 128], H[:, kc, :w], start=(kc == 0), stop=(kc == KC - 1), r=True)
                return p

            def stage():
                t = st_t[sti[0] % 4]
                sti[0] += 1
                return t

            nblk = 0
            for (t0, w, rowlen) in blocks_of(True):
                col = 1 if t0 == 0 else 0
                c.load(X[:, :, :w], xsrc[:, :, t0:t0 + w], dram_r="xT1")
                self.norm_mod(X, H, mv["A1"], mv["B1"], col, w, (xa[0], xa[1], xb[0]))
                if nblk == 0:
                    self.dbg(f"X{l}", X.full(), [128, KC, NB])
                ntt = w // 128
                wcache = {}

                def getw(key, src_ap):
                    if RES:
                        return Wres[key]
                    if key not in wcache:
                        wcache[key] = next_w(src_ap)
                    return wcache[key]

                def conv3_g(out, x, wk):
                    xv = x[:, :w].re("p (r t) -> p r t", t=rowlen)
                    ov = out[:, :w].re("p (r t) -> p r t", t=rowlen)
                    c.ts(out[:, :w], x[:, :w], wk[:, 1:2], ALU.mult)
                    yield
                    c.stt(ov[:, :, 1:rowlen], xv[:, :, 0:rowlen - 1], wk[:, 0:1], ov[:, :, 1:rowlen], ALU.mult, ALU.add)
                    yield
                    c.stt(ov[:, :, 0:rowlen - 1], xv[:, :, 1:rowlen], wk[:, 2:3], ov[:, :, 0:rowlen - 1], ALU.mult, ALU.add)
                    yield

                def u_qk(gi, j):
                    nm, scale = (("qT", 128.0 ** -0.5), ("kT", 1.0))[gi]

                    def g(sl):
                        wt = getw(("in", gi), win[gi])
                        p, p2, a_, b2 = PSs[2 * sl], PSs[2 * sl + 1], xa[sl], xb[sl]
                        d2 = kfm[j] if gi == 1 else stq[sl]
                        proj(wt, j, w, p)
                        yield
                        c.copy(a_[:, :w], p[:, :w], E=c.act)
                        yield
                        yield from conv3_g(b2, a_, dncw[:, gi * 4 + j, :])
                        c.act_fn(b2[:, :w], b2[:, :w], AF.Silu)
                        yield
                        c.tt(a_[:, :w], b2[:, :w], b2[:, :w], ALU.mult)
                        yield
                        c.mm(p2[:, :w], K["ones"].full(), a_[:, :w])
                        yield
                        c.act_fn(a_[:, :w], p2[:, :w], AF.Sqrt, bias=K["eps"].full(), scale=1.0)
                        yield
                        c.recip(a_[:, :w], a_[:, :w])
                        yield
                        c.stt(d2[:, :w], b2[:, :w], scale, a_[:, :w], ALU.mult, ALU.mult)
                        c.store(S[nm][j, :, t0:t0 + w], d2[:, :w], dram_w=nm)
                    return g

                def u_tok(src, dst_name, tt_):
                    def g(sl):
                        tk = tok[sl]
                        for j in range(4):
                            p3 = PSs[2 * sl + (j % 2)]
                            c.transpose(p3[:, 0:128], src[j][:, tt_ * 128:(tt_ + 1) * 128], K["ident"].full())
                            yield
                            c.copy(tk[:, j, :], p3[:, 0:128], E=(c.act if j % 2 else c.dve))
                            yield
                        c.store(S[dst_name][t0 + tt_ * 128:t0 + (tt_ + 1) * 128, :, :], tk.full(), dram_w=dst_name)
                    return g

                def u_v(j):
                    def g(sl):
                        wt = getw(("in", 2), win[2])
                        p, a_ = PSs[2 * sl], xa[sl]
                        proj(wt, j, w, p)
                        yield
                        c.copy(a_[:, :w], p[:, :w], E=c.act)
                        yield
                        yield from conv3_g(vfm[j], a_, dncw[:, 8 + j, :])
                        c.act_fn(vfm[j][:, :w], vfm[j][:, :w], AF.Silu)
                    return g

                def u_z(j):
                    def g(sl):
                        wt = getw(("in", 3), win[3])
                        p, d2 = PSs[2 * sl], stq[sl]
                        proj(wt, j, w, p)
                        yield
                        c.act_fn(d2[:, :w], p[:, :w], AF.Silu)
                        c.store(S["zT"][j, :, t0:t0 + w], d2[:, :w], dram_w="zT")
                    return g

                def u_ba():
                    def g(sl):
                        p = PSs[2 * sl]
                        for tt_ in range(ntt):
                            for kc in range(KC):
                                c.mm(p[:, tt_ * 16:(tt_ + 1) * 16], H[:, kc, tt_ * 128:(tt_ + 1) * 128], Wba[:, kc, :],
                                     start=(kc == 0), stop=(kc == KC - 1))
                        yield
                        pv = p[:, 0:ntt * 16].re("p (t f) -> p t f", f=16)
                        c.act_fn(bgt[:, :ntt, 0:8], pv[:, :, 0:8], AF.Sigmoid)
                        yield
                        for tt_ in range(ntt):
                            c.tt(bgx[:, tt_, :], pv[:, tt_, 8:16], dtb.full(), ALU.add)
                        yield
                        c.act_fn(bgx[:, :ntt, :], bgx[:, :ntt, :], AF.Exp)
                        yield
                        c.act_fn(bgx[:, :ntt, :], bgx[:, :ntt, :], AF.Ln, bias=1.0)
                        yield
                        for tt_ in range(ntt):
                            c.tt(bgt[:, tt_, 8:16], bgx[:, tt_, :], nalog.full(), ALU.mult)
                        c.store(S["bg"][t0:t0 + w, :].rearrange("(t p) f -> p t f", p=128), bgt[:, :ntt, :], dram_w="bg")
                    return g

                def u_scC(j):
                    def g(sl):
                        wt = getw(("in", 5), win[5])
                        p = PSs[2 * sl]
                        proj(wt, j, w, p)
                        yield
                        c.copy(scin[j][:, :w], p[:, :w], E=c.act)
                    return g

                def u_scX(j):
                    def g(sl):
                        wt = getw(("in", 6), win[6])
                        p, a_ = PSs[2 * sl], xa[sl]
                        proj(wt, j, w, p)
                        yield
                        c.tt(a_[:, :w], p[:, :w], scin[j][:, :w], ALU.mult)
                        yield
                        yield from conv3_g(scin[j], a_, sccw[:, j, :])
                    return g

                def u_scB(j):
                    def g(sl):
                        wt = getw(("in", 4), win[4])
                        p = PSs[2 * sl]
                        proj(wt, j, w, p)
                        yield
                        c.tt(scf[j][:, :w].r(), p[:, :w], scin[j][:, :w], ALU.mult)
                    return g

                def u_gate(gidx, nm, mc):
                    def g(sl):
                        half, j = mc // 4, mc % 4
                        wt = getw(("g", gidx, half), wgate[gidx * 2 + half])
                        p, p2, d2 = PSs[2 * sl], PSs[2 * sl + 1], stq[sl]
                        proj(wt, j, w, p)
                        yield
                        if gidx == 1:
                            for kc in range(4):
                                c.mm(p2[:, :w], Wsc[:, kc, mc * 128:(mc + 1) * 128], scf[kc][:, :w],
                                     start=(kc == 0), stop=(kc == 3), r=True)
                            yield
                        c.act_fn(d2[:, :w], p[:, :w], AF.Sigmoid, bias=bgate[:, gidx * 8 + mc: gidx * 8 + mc + 1])
                        yield
                        if gidx == 1:
                            c.tt(d2[:, :w], d2[:, :w], p2[:, :w], ALU.mult)
                        c.store(S[nm][mc, :, t0:t0 + w], d2[:, :w], dram_w=nm)
                    return g

                def u_fn(j):
                    def g(sl):
                        wt = getw(("in", 7), win[7])
                        p, d2 = PSs[2 * sl], stq[sl]
                        proj(wt, j, w, p)
                        yield
                        c.copy(d2[:, :w], p[:, :w], E=(c.act if j % 2 else c.dve))
                        c.store(S["fnT"][j, :, t0:t0 + w], d2[:, :w], dram_w="fnT")
                    return g

                FL = "flush"
                units = [u_qk(0, j) for j in range(4)] + [u_qk(1, j) for j in range(4)] + [FL]
                units += [u_tok(kfm, "ktok", tt_) for tt_ in range(ntt)] + [u_v(j) for j in range(4)] + [FL]
                units += [u_tok(vfm, "vtok", tt_) for tt_ in range(ntt)] + [u_z(j) for j in range(4)] + [u_ba()]
                units += [u_scC(j) for j in range(4)] + [FL] + [u_scX(j) for j in range(4)] + [FL]
                units += [u_scB(j) for j in range(4)] + [FL]
                units += [u_gate(1, "m1T", mc) for mc in range(8)] + [u_gate(0, "g0T", mc) for mc in range(8)]
                units += [u_gate(2, "g2T", mc) for mc in range(8)] + [u_fn(j) for j in range(4)]
                self.run_units(units, 4)
                nblk += 1
                if self.max_blocks and nblk >= self.max_blocks:
                    break
            c.barrier()


    def phase_fnet(self, l):
        c, K, S = self.c, self.K, self.scr
        N2 = SEQ // 128
        with ExitStack() as ph:
            R0 = c.sbuf([128, 512], name="fR0", stack=ph)
            c.load(R0.full(), self.inp["R0"])
            C128 = c.sbuf([128, 128], name="fC128", stack=ph)
            c.load(C128.full(), self.inp["C128"])
            S128 = c.sbuf([128, 128], name="fS128", stack=ph)
            c.load(S128.full(), self.inp["S128"])
            tw = {}
            for nm in ("twr", "twi", "ntwi"):
                tw[nm] = c.sbuf([128, N2], name="f" + nm, stack=ph)
                c.load(tw[nm].full(), self.inp[nm])
            G = c.sbuf([2 * N2, N2], name="fG", stack=ph)
            c.load(G.full(), self.inp["G64"])
            UT = c.sbuf([128, SEQ], name="fUT", stack=ph)
            YTt = c.sbuf([128, SEQ], name="fYT", stack=ph)
            A2 = c.sbuf([2 * N2, 128, 128], name="fA2", stack=ph)
            Zs = [c.sbuf([128, 512], name=f"fZs{i}", stack=ph) for i in range(2)]
            Apt = [c.sbuf([128, 2, 128], name=f"fAp{i}", stack=ph) for i in range(2)]
            t12 = [c.sbuf([128, 2, 128], name=f"ft12{i}", stack=ph) for i in range(2)]
            for g in range(4):
                c.load(UT.full(), S["fnT"][g, :, CTX:T], dram_r="fnT")
                UTv = UT.full().re("p (n1 n2) -> p n1 n2", n2=N2)
                apd = S["Ap"][g].rearrange("(ri n2) k c -> k ri n2 c", ri=2)
                for n2 in range(N2):
                    pa = self.ps()
                    c.mm(pa.full(), UTv[:, :, n2], R0.full())
                    zs = Zs[n2 % 2]
                    c.copy(zs.full(), pa.full(), E=c.act)
                    pb = self.ps()
                    c.mm(pb[:, 0:256], C128.full(), zs[:, 0:256], start=True, stop=False)
                    c.mm(pb[:, 0:256], S128.full(), zs[:, 256:512], start=False, stop=True)
                    ap_, tt_ = Apt[n2 % 2], t12[n2 % 2]
                    c.ts(tt_[:, 0, :], pb[:, 0:128], tw["twr"][:, n2:n2 + 1], ALU.mult)
                    c.ts(tt_[:, 1, :], pb[:, 0:128], tw["twi"][:, n2:n2 + 1], ALU.mult)
                    c.stt(ap_[:, 0, :], pb[:, 128:256], tw["ntwi"][:, n2:n2 + 1], tt_[:, 0, :], ALU.mult, ALU.add)
                    c.stt(ap_[:, 1, :], pb[:, 128:256], tw["twr"][:, n2:n2 + 1], tt_[:, 1, :], ALU.mult, ALU.add)
                    c.store(apd[:, :, n2, :], ap_.full(), dram_w="Ap")
                c.load(A2.full(), S["Ap"][g], dram_r="Ap")
                YTv = YTt.full().re("p (k2 k1) -> p k1 k2", k1=128)
                kb = max(1, 512 // N2)
                kb = min(kb, 128)
                for k1b in range(128 // kb):
                    pc = self.ps()
                    for i in range(kb):
                        k1 = k1b * kb + i
                        c.mm(pc[:, i * N2:(i + 1) * N2], A2[:, k1, :], G.full())
                    c.copy(YTv[:, k1b * kb:(k1b + 1) * kb, :], pc[:, 0:kb * N2].re("p (k1 k2) -> p k1 k2", k2=N2),
                           E=(c.act if k1b % 2 else c.dve))
                c.store(S["YT"][g, :, CTX:T], YTt.full(), dram_w="YT")
            c.barrier()
        if l == 0:
            with ExitStack() as ph:
                R0c = c.sbuf([128, 256], name="fR0c", stack=ph)
                c.load(R0c.full(), self.inp["R0c"])
                C256 = c.sbuf([128, 2, 256], name="fC256", stack=ph)
                c.load(C256.full(), self.inp["C256"].rearrange("(a p) k -> p a k", p=128))
                S256 = c.sbuf([128, 2, 256], name="fS256", stack=ph)
                c.load(S256.full(), self.inp["S256"].rearrange("(a p) k -> p a k", p=128))
                UTc = c.sbuf([128, 256], name="fUTc", stack=ph)
                Zc = [c.sbuf([128, 256], name=f"fZc{i}", stack=ph) for i in range(2)]
                Yc = c.sbuf([128, 256], name="fYc", stack=ph)
                for g in range(4):
                    c.load(UTc.full(), S["fnT"][g, :, 0:CTX], dram_r="fnT")
                    for tt_ in range(2):
                        pa = self.ps()
                        c.mm(pa[:, 0:256], UTc[:, tt_ * 128:(tt_ + 1) * 128], R0c.full())
                        c.copy(Zc[tt_].full(), pa[:, 0:256], E=c.act)
                    py = self.ps()
                    for tt_ in range(2):
                        c.mm(py[:, 0:256], Zc[tt_][:, 0:128], C256[:, tt_, :], start=(tt_ == 0), stop=False)
                        c.mm(py[:, 0:256], Zc[tt_][:, 128:256], S256[:, tt_, :], start=False, stop=(tt_ == 1))
                    c.copy(Yc.full(), py[:, 0:256])
                    c.store(S["YT"][g, :, 0:CTX], Yc.full(), dram_w="YT")
                c.barrier()

    def phase_dn(self, l):
        c, K, S = self.c, self.K, self.scr
        nblk = T // 128
        orders = [list(range(nblk)), [1, 0] + list(range(nblk - 1, 1, -1))]
        pos = [{b: i for i, b in enumerate(o)} for o in orders]
        corders = [(0, 1), (1, 0)]
        Us = [K["Uf"], K["Ub"]]
        NSs = [K["NSf"], K["NSb"]]
        NIs = [K["NIf"], K["NIb"]]
        kTv = S["kT"].rearrange("h p t -> p h t")
        qTv = S["qT"].rearrange("h p t -> p h t")
        zTv = S["zT"].rearrange("h p t -> p h t")
        yTv = S["ydnT"].rearrange("h p t -> p h t")
        otk = [S["otok"], S["otokb"]]
        otn = ["otok", "otokb"]
        HD = [(dr, h) for dr in range(2) for h in range(4)]
        with ExitStack() as ph:
            sb = lambda shp, nm: c.sbuf(shp, name=f"d{nm}", stack=ph)
            D2 = lambda f: [f(dr) for dr in range(2)]
            St = D2(lambda dr: [sb([128, 128], f"S{dr}{h}") for h in range(4)])
            for dr, h in HD:
                c.memset(St[dr][h].full(), 0.0)
            onorm = sb([128, 1], "onorm")
            c.load(onorm.full(), self.inp["onormT"][l])
            kT4 = D2(lambda dr: [sb([128, 4, 128], f"kT4{dr}{i}") for i in range(2)])
            qT4 = D2(lambda dr: [sb([128, 4, 128], f"qT4{dr}{i}") for i in range(2)])
            ktb = D2(lambda dr: [sb([128, 4, 128], f"ktb{dr}{i}") for i in range(2)])
            vtb = D2(lambda dr: [sb([128, 4, 128], f"vtb{dr}{i}") for i in range(2)])
            kt1 = D2(lambda dr: [sb([64, 4, 128], f"kt1{dr}{i}") for i in range(2)])
            bgb = D2(lambda dr: [sb([128, 16], f"bgb{dr}{i}") for i in range(2)])
            zb = D2(lambda dr: sb([128, 4, 128], f"zb{dr}"))
            ofl = D2(lambda dr: [sb([64, 4, 128], f"of{dr}{cc}") for cc in range(2)])
            sm = D2(lambda dr: sb([128, 20], f"sm{dr}"))
            esm = D2(lambda dr: sb([128, 20], f"esm{dr}"))
            nsm = D2(lambda dr: sb([128, 20], f"nsm{dr}"))
            kds = D2(lambda dr: sb([64, 8], f"kds{dr}"))
            bexp = D2(lambda dr: sb([128, 4], f"bexp{dr}"))
            nbeta = D2(lambda dr: sb([128, 4], f"nbeta{dr}"))
            rhsG = D2(lambda dr: [sb([128, 128], f"rhsG{dr}{h}") for h in range(4)])
            Dst = D2(lambda dr: [sb([128, 128], f"Dst{dr}{h}") for h in range(4)])
            Nt = D2(lambda dr: [[sb([128, 128], f"N{dr}{h}{i}") for i in range(2)] for h in range(4)])
            MS = D2(lambda dr: [[sb([128, 256], f"MS{dr}{h}{i}") for i in range(2)] for h in range(4)])
            TT = D2(lambda dr: [sb([128, 128], f"TT{dr}{h}") for h in range(4)])
            Rv = D2(lambda dr: [sb([128, 128], f"Rv{dr}{h}") for h in range(4)])
            Rw = D2(lambda dr: [sb([128, 128], f"Rw{dr}{h}") for h in range(4)])
            ut = D2(lambda dr: [[sb([64, 128], f"u{dr}{h}{cc}") for cc in range(2)] for h in range(4)])
            wT = D2(lambda dr: [sb([128, 128], f"wT{dr}{h}") for h in range(4)])
            DT = D2(lambda dr: [[sb([64, 64], f"DT{dr}{h}{cc}") for cc in range(2)] for h in range(4)])
            aT = D2(lambda dr: [[sb([64, 64], f"aT{dr}{h}{cc}") for cc in range(2)] for h in range(4)])
            kd = D2(lambda dr: [[sb([64, 128], f"kd{dr}{h}{cc}") for cc in range(2)] for h in range(4)])
            vn = D2(lambda dr: [sb([64, 128], f"vn{dr}{h}") for h in range(4)])
            o1 = D2(lambda dr: [sb([64, 128], f"o1{dr}{h}") for h in range(4)])
            ob = D2(lambda dr: [[sb([64, 4, 128], f"ob{dr}{i}{cc}") for cc in range(2)] for i in range(2)])
            osq = sb([64, 4, 128], "osq")
            oss = sb([64, 4], "oss")
            yT = D2(lambda dr: sb([128, 4, 128], f"yT{dr}"))
            ident, ones, negones = K["ident"], K["ones"], K["negones"]
            BK = self.PS
            import os
            for it in range(min(nblk, int(os.environ.get("DN_ITERS", "100000")))):
                par = it % 2
                blk = [orders[0][it], orders[1][it]]
                comb = [pos[1 - dr][blk[dr]] < it for dr in range(2)]
                k4, q4, kb_, vb_, k1_, bg_, beta, gD = ([None, None] for _ in range(8))
                for dr in range(2):
                    t0 = blk[dr] * 128
                    k4[dr], q4[dr], kb_[dr], vb_[dr], k1_[dr], bg_[dr] = (kT4[dr][par], qT4[dr][par], ktb[dr][par],
                                                                          vtb[dr][par], kt1[dr][par], bgb[dr][par])
                    c.load(k4[dr].full(), kTv[:, :, t0:t0 + 128], dram_r="kT")
                    c.load(q4[dr].full(), qTv[:, :, t0:t0 + 128], dram_r="qT")
                    c.load(kb_[dr].full(), S["ktok"][t0:t0 + 128], dram_r="ktok")
                    c.load(vb_[dr].full(), S["vtok"][t0:t0 + 128], dram_r="vtok")
                    c.load(k1_[dr].full(), S["ktok"][t0 + 64:t0 + 128], dram_r="ktok")
                    c.load(bg_[dr].full(), S["bg"][t0:t0 + 128, :], dram_r="bg")
                    if comb[dr]:
                        c.load(zb[dr].full(), zTv[:, :, t0:t0 + 128], dram_r="zT")
                        for cc in range(2):
                            c.load(ofl[dr][cc].full(), otk[1 - dr][t0 + cc * 64:t0 + (cc + 1) * 64], dram_r=otn[1 - dr])
                    beta[dr] = bg_[dr][:, dr * 4:(dr + 1) * 4]
                    gD[dr] = bg_[dr][:, 8 + dr * 4:8 + (dr + 1) * 4]
                for dr in range(2):
                    U = Us[dr]
                    p = self.ps()
                    c.mm(p[:, 0:4], U.full(), gD[dr])
                    for cc in range(2):
                        c.mm(p[0:64, 4 + cc * 4:8 + cc * 4], U[:, cc * 64:(cc + 1) * 64], gD[dr])
                        c.mm(p[:, 12 + cc * 4:16 + cc * 4], K["SelC"][:, cc, :], gD[dr])
                    c.copy(sm[dr].full(), p[:, 0:20])
                    c.act_fn(esm[dr].full(), p[:, 0:20], AF.Exp)
                    c.ts(nsm[dr].full(), sm[dr].full(), -1.0, ALU.mult)
                    for cc in range(2):
                        c.tt(kds[dr][:, cc * 4:(cc + 1) * 4], sm[dr][0:64, 12 + cc * 4:16 + cc * 4],
                             sm[dr][0:64, 4 + cc * 4:8 + cc * 4], ALU.subtract)
                    c.act_fn(kds[dr].full(), kds[dr].full(), AF.Exp)
                    c.tt(bexp[dr].full(), beta[dr], esm[dr][:, 0:4], ALU.mult)
                    c.ts(nbeta[dr].full(), beta[dr], -1.0, ALU.mult)
                for i, (dr, h) in enumerate(HD):
                    bk = BK[i]
                    c.mm(bk[:, 0:128], k4[dr][:, h, :], k4[dr][:, h, :])
                    c.ts(rhsG[dr][h].full(), Us[dr].full(), gD[dr][:, h:h + 1], ALU.mult)
                    c.mm(bk[:, 128:256], negones.full(), rhsG[dr][h].full(), start=True, stop=False)
                    c.mm(bk[:, 128:256], ident.full(), NSs[dr].full(), start=False, stop=True)
                for i, (dr, h) in enumerate(HD):
                    bk = BK[i]
                    c.act_fn(Dst[dr][h].full(), bk[:, 128:256], AF.Exp, bias=sm[dr][:, h:h + 1])
                    c.stt(Nt[dr][h][0].full(), bk[:, 0:128], nbeta[dr][:, h:h + 1], Dst[dr][h].full(), ALU.mult, ALU.mult)
                for i, (dr, h) in enumerate(HD):
                    bk = BK[i]
                    c.transpose(bk[:, 0:128], Nt[dr][h][0].full(), ident.full())
                for i, (dr, h) in enumerate(HD):
                    bk = BK[i]
                    c.copy(MS[dr][h][0][:, 0:128], bk[:, 0:128], E=c.act)
                    c.tt(MS[dr][h][0][:, 128:256], bk[:, 0:128], ident.full(), ALU.add)
                for k in range(6):
                    a, bn = k % 2, (k + 1) % 2
                    for dr in range(2):
                        pms, pns = [], []
                        for h in range(4):
                            N_, MS_ = Nt[dr][h][a], MS[dr][h][a]
                            pm = self.ps()
                            if k == 0:
                                c.mm(pm[:, 0:128], N_.full(), MS_[:, 0:128])
                            elif k <= 3:
                                c.mm(pm[:, 0:256], N_.full(), MS_.full())
                            else:
                                c.mm(pm[:, 0:128], N_.full(), MS_[:, 128:256])
                            pms.append(pm)
                            if k <= 4:
                                pn = self.ps()
                                c.mm(pn[:, 0:128], MS_[:, 0:128], N_.full())
                                pns.append(pn)
                        for h in range(4):
                            MSa, MSb = MS[dr][h][a], MS[dr][h][bn]
                            if k == 0:
                                c.copy(MSb[:, 0:128], pms[h][:, 0:128], E=c.act)
                                c.copy(MSb[:, 128:256], MSa[:, 128:256])
                            elif k <= 3:
                                c.copy(MSb[:, 0:128], pms[h][:, 0:128], E=c.act)
                                c.tt(MSb[:, 128:256], pms[h][:, 128:256], MSa[:, 128:256], ALU.add)
                            elif k == 4:
                                c.tt(MSb[:, 128:256], pms[h][:, 0:128], MSa[:, 128:256], ALU.add)
                            else:
                                c.tt(TT[dr][h].full(), pms[h][:, 0:128], MSa[:, 128:256], ALU.add)
                            if k <= 4:
                                c.copy(Nt[dr][h][bn].full(), pns[h][:, 0:128], E=c.act)
                for dr in range(2):
                    for h in range(4):
                        c.ts(Rv[dr][h].full(), vb_[dr][:, h, :], beta[dr][:, h:h + 1], ALU.mult, 1.0, ALU.mult, E=c.pool)
                        c.ts(Rw[dr][h].full(), kb_[dr][:, h, :], bexp[dr][:, h:h + 1], ALU.mult, 1.0, ALU.mult, E=c.pool)
                        for cc in range(2):
                            pu = self.ps()
                            c.mm(pu[0:64, 0:128], TT[dr][h][:, cc * 64:(cc + 1) * 64], Rv[dr][h].full())
                            c.copy(ut[dr][h][cc].full(), pu[0:64, 0:128], E=c.act)
                        pw = self.ps()
                        c.mm(pw[:, 0:128], Rw[dr][h].full(), TT[dr][h].full())
                        c.copy(wT[dr][h].full(), pw[:, 0:128], E=c.act)
                        for cc in range(2):
                            cs = slice(cc * 64, (cc + 1) * 64)
                            pq = self.ps()
                            c.mm(pq[0:64, 0:64], k4[dr][:, h, cs], q4[dr][:, h, cs])
                            pg = self.ps()
                            c.mm(pg[0:64, 0:64], ones[:, 0:64], rhsG[dr][h][:, cs], start=True, stop=False)
                            c.mm(pg[0:64, 0:64], ident[0:64, 0:64], NIs[dr].full(), start=False, stop=True)
                            c.act_fn(DT[dr][h][cc].full(), pg[0:64, 0:64], AF.Exp, bias=nsm[dr][0:64, 4 + cc * 4 + h:5 + cc * 4 + h])
                            c.tt(aT[dr][h][cc].full(), pq[0:64, 0:64], DT[dr][h][cc].full(), ALU.mult)
                            ksrc = kb_[dr][0:64, h, :] if cc == 0 else k1_[dr][:, h, :]
                            c.ts(kd[dr][h][cc].full(), ksrc, kds[dr][:, cc * 4 + h:cc * 4 + h + 1], ALU.mult, 1.0, ALU.mult, E=c.pool)
                for step in range(2):
                    for dr in range(2):
                        cc = corders[dr][step]
                        cs = slice(cc * 64, (cc + 1) * 64)
                        pws, pos_ = [], []
                        for h in range(4):
                            pw = self.ps()
                            c.mm(pw[0:64, 0:128], wT[dr][h][:, cs], St[dr][h].full())
                            po = self.ps()
                            c.mm(po[0:64, 0:128], q4[dr][:, h, cs], St[dr][h].full())
                            pws.append(pw)
                            pos_.append(po)
                        for h in range(4):
                            c.tt(vn[dr][h].full(), ut[dr][h][cc].full(), pws[h][0:64, 0:128], ALU.subtract)
                            c.act_fn(o1[dr][h].full(), pos_[h][0:64, 0:128], AF.Copy, scale=esm[dr][0:64, 4 + cc * 4 + h:5 + cc * 4 + h]) \
                                if False else c.ts(o1[dr][h].full(), pos_[h][0:64, 0:128], esm[dr][0:64, 4 + cc * 4 + h:5 + cc * 4 + h], ALU.mult)
                        pas, pss = [], []
                        for h in range(4):
                            pa = self.ps()
                            c.mm(pa[0:64, 0:128], aT[dr][h][cc].full(), vn[dr][h].full())
                            psn = self.ps()
                            c.mm(psn[:, 0:128], kd[dr][h][cc].full(), vn[dr][h].full())
                            pas.append(pa)
                            pss.append(psn)
                        for h in range(4):
                            c.tt(ob[dr][par][cc][:, h, :], o1[dr][h].full(), pas[h][0:64, 0:128], ALU.add)
                            c.stt(St[dr][h].full(), St[dr][h].full(), esm[dr][:, 12 + cc * 4 + h:13 + cc * 4 + h], pss[h][:, 0:128],
                                  ALU.mult, ALU.add)
                for dr in range(2):
                    t0 = blk[dr] * 128
                    if not comb[dr]:
                        for cc in range(2):
                            c.store(otk[dr][t0 + cc * 64:t0 + (cc + 1) * 64], ob[dr][par][cc].full(), dram_w=otn[dr])
                        continue
                    for cc in range(2):
                        o_ = ob[dr][par][cc]
                        c.tt(o_.full(), o_.full(), ofl[dr][cc].full(), ALU.add)
                        c.tt(osq.full(), o_.full(), o_.full(), ALU.mult)
                        c.op(c.dve, lambda: self.nc.vector.tensor_reduce(oss.h[:], osq.h[:], mybir.AxisListType.X, ALU.add),
                             [osq.full()], [oss.full()])
                        c.act_fn(oss.full(), oss.full(), AF.Sqrt, bias=K["eps"][0:64, :], scale=1.0 / 128)
                        c.recip(oss.full(), oss.full())
                        for h in range(4):
                            c.ts(o_[:, h, :], o_[:, h, :], oss[:, h:h + 1], ALU.mult)
                            pt = self.ps()
                            c.transpose(pt[:, 0:64], o_[:, h, :], ident[0:64, 0:64])
                            c.stt(yT[dr][:, h, cc * 64:(cc + 1) * 64], pt[:, 0:64], onorm.full(),
                                  zb[dr][:, h, cc * 64:(cc + 1) * 64], ALU.mult, ALU.mult)
                    c.store(yTv[:, :, t0:t0 + 128], yT[dr].full(), dram_w="ydnT")
            c.barrier()

    def phase_p4(self, l):
        c, K, S = self.c, self.K, self.scr
        mv = self.modv[l]
        last = l == DEPTH - 1
        xsrc = (self.inp["xT0"] if l == 0 else S["xT1"]).rearrange("(kc p) t -> p kc t", p=128)
        xdst = S["xT1"].rearrange("(kc p) t -> p kc t", p=128)
        odst = self.outT.rearrange("(kc p) t -> p kc t", p=128)
        wo = self.inp["w_o_t"][l]
        wf1 = self.inp["w_ffn_in_t"][l]
        wf2 = self.inp["w_ffn_out_t"][l]
        with ExitStack() as ph:
            sb = lambda shp, nm: c.sbuf(shp, name=f"q{nm}", stack=ph)
            X = sb([128, KC, NB], "X")
            Hb = c.sbuf([128, KC, NB], dtype=MMT, name="qHb", stack=ph)
            hid = c.sbuf([128, FC, NB], dtype=MMT, name="qhid", stack=ph)
            NW4 = 6 if MMT == BF16 else 3
            W = [c.sbuf([128, KC, 512], dtype=MMT, name=f"qW{i}", stack=ph) for i in range(NW4)]
            W2 = [c.sbuf([128, FC, 128], dtype=MMT, name=f"qW2{i}", stack=ph) for i in range(2)]
            Yd = c.sbuf([128, 4, NB], dtype=MMT, name="qYd", stack=ph)
            Yf = c.sbuf([128, 4, NB], dtype=MMT, name="qYf", stack=ph)
            gm = [[sb([128, NB], f"gm{i}{j}") for j in range(3)] for i in range(2)]
            tmp = [sb([128, NB], f"tmp{i}") for i in range(3)]
            sq0, sq1, rstd = sb([128, NB], "sq0"), sb([128, NB], "sq1"), sb([128, NB], "rstd")
            fing = sb([128, 8], "fing")
            c.load(fing.full(), self.inp["fingT"])
            wi = [0]

            def next_w(src_ap):
                wt = W[wi[0] % NW4]
                wi[0] += 1
                c.load_r(wt.full(), src_ap, dram_r="wb")
                return wt

            PRE = (MMT == BF16)
            if PRE:
                jobs = [(self.inp["w_dn_t"][l], S["wb_dn"]), (self.inp["w_fn_t"][l], S["wb_fn"])]
                jobs += [(wo[h_], S["wb_o"][h_]) for h_ in range(2)]
                jobs += [(wf1[g_], S["wb_f1"][g_]) for g_ in range(FC // 2)]
                for (src_, dst_) in jobs:
                    wt = W[wi[0] % NW4]
                    wi[0] += 1
                    c.load_r(wt.full(), src_)
                    c.store(dst_, wt.full(), dram_w="wb")
                for m_ in range(8):
                    w2 = W2[m_ % 2]
                    c.load_r(w2.full(), wf2[m_])
                    c.store(S["wb_f2"][m_], w2.full(), dram_w="wb")
                wdn_s, wfn_s, wo, wf1, wf2 = S["wb_dn"], S["wb_fn"], S["wb_o"], S["wb_f1"], S["wb_f2"]
            else:
                wdn_s, wfn_s = self.inp["w_dn_t"][l], self.inp["w_fn_t"][l]
            nblk = 0
            for (t0, w, _) in blocks_of(not last):
                col = 1 if t0 == 0 else 0
                c.load(X[:, :, :w], xsrc[:, :, t0:t0 + w], dram_r="xT1")
                c.load_r(Yd[:, :, :w], S["ydnT"].rearrange("h p t -> p h t")[:, :, t0:t0 + w], dram_r="ydnT")
                c.load_r(Yf[:, :, :w], S["YT"].rearrange("h p t -> p h t")[:, :, t0:t0 + w], dram_r="YT")
                Wdn = next_w(wdn_s)
                Wfn = next_w(wfn_s)
                for mc in range(8):
                    g_ = gm[mc % 2]
                    c.load(g_[0][:, :w], S["g0T"][mc, :, t0:t0 + w], dram_r="g0T")
                    c.load(g_[1][:, :w], S["g2T"][mc, :, t0:t0 + w], dram_r="g2T")
                    c.load(g_[2][:, :w], S["m1T"][mc, :, t0:t0 + w], dram_r="m1T")
                    pd = self.ps()
                    pf = self.ps()
                    for kc in range(4):
                        c.mm(pd[:, :w], Wdn[:, (mc // 4) * 4 + kc, (mc % 4) * 128:(mc % 4 + 1) * 128], Yd[:, kc, :w], start=(kc == 0), stop=(kc == 3), r=True)
                    for kc in range(4):
                        c.mm(pf[:, :w], Wfn[:, (mc // 4) * 4 + kc, (mc % 4) * 128:(mc % 4 + 1) * 128], Yf[:, kc, :w], start=(kc == 0), stop=(kc == 3), r=True)
                    c.tt(g_[0][:, :w], g_[0][:, :w], pd[:, :w], ALU.mult)
                    c.tt(g_[1][:, :w], g_[1][:, :w], pf[:, :w], ALU.mult)
                    c.tt(g_[0][:, :w], g_[0][:, :w], g_[1][:, :w], ALU.add)
                    c.tt(Hb[:, mc, :w].r(), g_[0][:, :w], g_[2][:, :w], ALU.add)
                for half in range(2):
                    wt = next_w(wo[half])
                    for j in range(4):
                        mc = half * 4 + j
                        p = self.ps()
                        for kc in range(KC):
                            c.mm(p[:, :w], wt[:, kc, j * 128:(j + 1) * 128], Hb[:, kc, :w], start=(kc == 0), stop=(kc == KC - 1), r=True)
                        c.stt(X[:, mc, :w], p[:, :w], mv["G1"][:, mc, col:col + 1], X[:, mc, :w], ALU.mult, ALU.add)
                if nblk == 1:
                    self.dbg(f"xmid{l}", X.full(), [128, KC, NB])
                self.norm_mod(X, Hb, mv["A2"], mv["B2"], col, w, (sq0, sq1, rstd))
                for gi in range(FC // 2):
                    wt = next_w(wf1[gi])
                    for j in range(2):
                        fcx = gi * 2 + j
                        pa = self.ps()
                        pb = self.ps()
                        for kc in range(KC):
                            c.mm(pa[:, :w], wt[:, kc, j * 128:(j + 1) * 128], Hb[:, kc, :w], start=(kc == 0), stop=(kc == KC - 1), r=True)
                        for kc in range(KC):
                            c.mm(pb[:, :w], wt[:, kc, 256 + j * 128:256 + (j + 1) * 128], Hb[:, kc, :w], start=(kc == 0), stop=(kc == KC - 1), r=True)
                        tm = tmp[fcx % 3]
                        c.act_fn(tm[:, :w], pa[:, :w], AF.Silu)
                        c.tt(hid[:, fcx, :w].r(), tm[:, :w], pb[:, :w], ALU.mult)
                for mc in range(8):
                    w2 = W2[mc % 2]
                    c.load_r(w2.full(), wf2[mc], dram_r="wb")
                    p = self.ps()
                    for fcx in range(FC):
                        c.mm(p[:, :w], w2[:, fcx, :], hid[:, fcx, :w], start=(fcx == 0), stop=(fcx == FC - 1), r=True)
                    c.stt(X[:, mc, :w], p[:, :w], mv["G2"][:, mc, col:col + 1], X[:, mc, :w], ALU.mult, ALU.add)
                if not last:
                    c.store(xdst[:, :, t0:t0 + w], X[:, :, :w], dram_w="xT1")
                else:
                    p = self.ps()
                    for kc in range(KC):
                        sq = sq0 if kc % 2 == 0 else sq1
                        c.tt(sq[:, :w], X[:, kc, :w], X[:, kc, :w], ALU.mult)
                        c.mm(p[:, :w], K["ones"].full(), sq[:, :w], start=(kc == 0), stop=(kc == KC - 1))
                    c.act_fn(rstd[:, :w], p[:, :w], AF.Sqrt, bias=K["eps"].full(), scale=1.0 / D)
                    c.recip(rstd[:, :w], rstd[:, :w])
                    for kc in range(KC):
                        c.stt(X[:, kc, :w], X[:, kc, :w], fing[:, kc:kc + 1], rstd[:, :w], ALU.mult, ALU.mult)
                    c.store(odst[:, :, t0 - CTX:t0 - CTX + w], X[:, :, :w])
                nblk += 1
                if self.max_blocks and nblk >= self.max_blocks:
                    break
            c.barrier()

_CACHE = {}


def kernel(**inputs):
    inp = {k: np.asarray(v) for k, v in inputs.items()}
    consts = make_consts()
    w = host_layout(inp)
    if "nc" not in _CACHE:
        dry = Builder()
        dry.build()
        _CACHE["nc"] = Builder(needed=dry.c.needed).build()
    nc = _CACHE["nc"]
    in_maps = []
    for b in range(NCORES):
        xT0 = np.ascontiguousarray(np.concatenate([inp["ctx"][b], inp["x"][b]], 0).T, dtype=np.float32)
        cc = np.stack([inp["c"][b].reshape(8, 128).T, inp["c_ctx"].reshape(8, 128).T], -1).astype(np.float32)
        m = dict(xT0=xT0, ccT=np.ascontiguousarray(cc))
        m.update(consts)
        m.update(w)
        in_maps.append(m)
    res = run_bass_kernel_spmd(nc, in_maps, core_ids=list(range(NCORES)))
    out = np.stack([np.ascontiguousarray(r["outT"].T) for r in res.results], 0)
    return out.astype(np.float32)
```
